# Optimizing a Trainium2 kernel written in Bass

```python
import math
import jax
import jax.numpy as jnp
from jax import lax
import numpy as np

D_MODEL = 2048
BATCH = 2
SEQ = 8192
DEPTH = 4

GRID_W = 64
CTX_LEN = 256
N_MIXERS = 3
ALPHA = (2.0 * DEPTH) ** 0.25
BETA = (8.0 * DEPTH) ** -0.25
LN_EPS = 1e-6
Q_BLOCK = 128
ROPE_DIM = 64
ROPE_BASE = 10000.0
GM_CHUNK = 128
GM_DIM = D_MODEL
GM_GROUPS = GM_DIM // 128
GM_GDIM = GM_DIM // GM_GROUPS
MLA_HEADS = D_MODEL // 128
MLA_Q_RANK = D_MODEL // 4
MLA_KV_RANK = D_MODEL // 4
MLA_NOPE = 128
MLA_ROPE = ROPE_DIM
MLA_V = 128
MLA_SCALE = (MLA_NOPE + MLA_ROPE) ** -0.5
DA_HEAD = ROPE_DIM
DA_HEADS = D_MODEL // (2 * DA_HEAD)
DA_SCALE = DA_HEAD ** -0.5
PEER_HEADS = 8
PEER_TOPK = 16
PEER_NKEYS = 128
PEER_EXPERTS = PEER_NKEYS * PEER_NKEYS
PEER_DK = 256
PEER_CHUNK = 128
N_LAYERS_A = len(range(0, DEPTH, N_MIXERS))
N_LAYERS_B = len(range(1, DEPTH, N_MIXERS))
N_LAYERS_C = len(range(2, DEPTH, N_MIXERS))

kernel_name = 'hybrid_dit_gmlp_mla_diffattn_peer'


def _layer_norm(x):
    xf = x.astype(jnp.float32)
    mu = jnp.mean(xf, axis=-1, keepdims=True)
    var = jnp.mean(jnp.square(xf - mu), axis=-1, keepdims=True)
    return ((xf - mu) * lax.rsqrt(var + LN_EPS)).astype(x.dtype)


def _rms_norm(x):
    xf = x.astype(jnp.float32)
    return (xf * lax.rsqrt(jnp.mean(jnp.square(xf), axis=-1, keepdims=True) + LN_EPS)).astype(x.dtype)


def _modulate(x, shift, scale):
    return _layer_norm(x) * (1.0 + scale) + shift


def _post_norm(x, y, g, b):
    return _layer_norm(ALPHA * x + y) * g + b


def axial_rope_tables(n_tokens):
    rows = n_tokens // GRID_W
    row = jnp.repeat(jnp.arange(rows, dtype=jnp.float32), GRID_W)
    col = jnp.tile(jnp.arange(GRID_W, dtype=jnp.float32), rows)
    nf = ROPE_DIM // 4
    inv = ROPE_BASE ** (-jnp.arange(nf, dtype=jnp.float32) / nf)
    ang_r = row[:, None] * inv
    ang_c = col[:, None] * inv
    return (jnp.cos(ang_r), jnp.sin(ang_r), jnp.cos(ang_c), jnp.sin(ang_c))


def _rope_half(x, cos, sin):
    x1, x2 = jnp.split(x, 2, axis=-1)
    return jnp.concatenate([x1 * cos - x2 * sin, x1 * sin + x2 * cos], axis=-1)


def axial_rope(x, rope):
    cos_r, sin_r, cos_c, sin_c = rope
    xf = x.astype(jnp.float32)
    xr, xc = jnp.split(xf, 2, axis=-1)
    out = jnp.concatenate([_rope_half(xr, cos_r, sin_r), _rope_half(xc, cos_c, sin_c)], axis=-1)
    return out.astype(x.dtype)


def _merge_heads(o):
    B, H, T, dv = o.shape
    return o.transpose(0, 2, 1, 3).reshape(B, T, H * dv)


def _over_query_blocks(fn, qs):
    B, H, S, _ = qs[0].shape
    nb = S // Q_BLOCK
    blocks = tuple(q.reshape(B, H, nb, Q_BLOCK, q.shape[-1]).transpose(2, 0, 1, 3, 4) for q in qs)
    o = lax.map(fn, blocks)
    return o.transpose(1, 2, 0, 3, 4).reshape(B, H, S, o.shape[-1])


def softmax_attention(q, k, v, scale):
    def blk(qs):
        (qb,) = qs
        s = jnp.einsum('bhqd,bhkd->bhqk', qb, k) * scale
        p = jax.nn.softmax(s.astype(jnp.float32), axis=-1)
        return jnp.einsum('bhqk,bhkd->bhqd', p.astype(v.dtype), v)
    return _over_query_blocks(blk, (q,))


def diff_attention(q1, q2, k1, k2, v, lam, scale):
    def blk(qs):
        a, b = qs
        p1 = jax.nn.softmax((jnp.einsum('bhqd,bhkd->bhqk', a, k1) * scale).astype(jnp.float32), axis=-1)
        p2 = jax.nn.softmax((jnp.einsum('bhqd,bhkd->bhqk', b, k2) * scale).astype(jnp.float32), axis=-1)
        return jnp.einsum('bhqk,bhkd->bhqd', (p1 - lam * p2).astype(v.dtype), v)
    return _over_query_blocks(blk, (q1, q2))


def gmlp_mixer(hl, hc, w_in, ln_g, ln_b, ws, bs, w_out):
    def mix(h):
        B, T, _ = h.shape
        z = jax.nn.gelu(h @ w_in)
        u, v = jnp.split(z, 2, axis=-1)
        v = _layer_norm(v) * ln_g + ln_b
        v = v.reshape(B, T // GM_CHUNK, GM_CHUNK, GM_GROUPS, GM_GDIM)
        sv = jnp.einsum('gpq,bnqgd->bnpgd', ws, v) + bs.T[:, :, None]
        return (u * sv.reshape(B, T, GM_DIM)) @ w_out
    return mix(hl), (mix(hc) if hc is not None else None)


def mla_mixer(hl, hc, rope, w_in, q_norm, kv_norm, w_uq, w_ukv, w_out, ctx_out):
    def down(h):
        return jnp.split(h @ w_in, [MLA_Q_RANK, MLA_Q_RANK + MLA_KV_RANK], axis=-1)

    def up_q(cq):
        B, T, _ = cq.shape
        q = (_rms_norm(cq) * q_norm) @ w_uq
        q = q.reshape(B, T, MLA_HEADS, MLA_NOPE + MLA_ROPE).transpose(0, 2, 1, 3)
        return jnp.split(q, [MLA_NOPE], axis=-1)

    def up_kv(ckv, k_rope):
        B, T, _ = ckv.shape
        kv = (_rms_norm(ckv) * kv_norm) @ w_ukv
        kv = kv.reshape(B, T, MLA_HEADS, MLA_NOPE + MLA_V).transpose(0, 2, 1, 3)
        k_nope, v = jnp.split(kv, [MLA_NOPE], axis=-1)
        k_rope = jnp.broadcast_to(k_rope, k_nope.shape[:-1] + (MLA_ROPE,))
        return jnp.concatenate([k_nope, k_rope], axis=-1), v

    cq_l, ckv_l, kr_l = down(hl)
    cq_c, ckv_c, kr_c = down(hc)
    qn_l, qr_l = up_q(cq_l)
    q_l = jnp.concatenate([qn_l, axial_rope(qr_l, rope)], axis=-1)
    k_l, v_l = up_kv(ckv_l, axial_rope(kr_l[:, None], rope))
    k_c, v_c = up_kv(ckv_c, kr_c[:, None])
    o_l = softmax_attention(q_l, jnp.concatenate([k_c, k_l], axis=2), jnp.concatenate([v_c, v_l], axis=2), MLA_SCALE)
    y_l = _merge_heads(o_l) @ w_out
    if ctx_out:
        q_c = jnp.concatenate(up_q(cq_c), axis=-1)
        y_c = _merge_heads(softmax_attention(q_c, k_c, v_c, MLA_SCALE)) @ w_out
    else:
        y_c = None
    return y_l, y_c


def diff_attn_mixer(hl, hc, rope, w_in, lam_p, subln, w_out, lam_init, ctx_out):
    def proj(h):
        B, T, _ = h.shape
        q, k, v = jnp.split(h @ w_in, 3, axis=-1)
        q = q.reshape(B, T, DA_HEADS, 2, DA_HEAD).transpose(3, 0, 2, 1, 4)
        k = k.reshape(B, T, DA_HEADS, 2, DA_HEAD).transpose(3, 0, 2, 1, 4)
        v = v.reshape(B, T, DA_HEADS, 2 * DA_HEAD).transpose(0, 2, 1, 3)
        return q, k, v

    lp = lam_p.astype(jnp.float32)
    lam = jnp.exp(jnp.sum(lp[0] * lp[1])) - jnp.exp(jnp.sum(lp[2] * lp[3])) + lam_init

    def out(o):
        o = _rms_norm(o) * subln * (1.0 - lam_init)
        return _merge_heads(o) @ w_out

    q_l, k_l, v_l = proj(hl)
    q_l = axial_rope(q_l, rope)
    k_l = axial_rope(k_l, rope)
    q_c, k_c, v_c = proj(hc)
    k_all = jnp.concatenate([k_c, k_l], axis=3)
    v_all = jnp.concatenate([v_c, v_l], axis=2)
    y_l = out(diff_attention(q_l[0], q_l[1], k_all[0], k_all[1], v_all, lam, DA_SCALE))
    y_c = out(diff_attention(q_c[0], q_c[1], k_c[0], k_c[1], v_c, lam, DA_SCALE)) if ctx_out else None
    return y_l, y_c


def peer_ffn(h, wq, k1, k2, u_tab, v_tab):
    B, T, D = h.shape
    q = (h @ wq).reshape(B, T, PEER_HEADS, 2, PEER_DK // 2)
    s1 = jnp.einsum('bthd,nd->bthn', q[..., 0, :], k1)
    s2 = jnp.einsum('bthd,nd->bthn', q[..., 1, :], k2)
    v1, i1 = lax.top_k(s1, PEER_TOPK)
    v2, i2 = lax.top_k(s2, PEER_TOPK)
    cand = (v1[..., :, None] + v2[..., None, :]).reshape(B, T, PEER_HEADS, PEER_TOPK * PEER_TOPK)
    cidx = (i1[..., :, None] * PEER_NKEYS + i2[..., None, :]).reshape(B, T, PEER_HEADS, PEER_TOPK * PEER_TOPK)
    best, sel = lax.top_k(cand, PEER_TOPK)
    eidx = jnp.take_along_axis(cidx, sel, axis=-1)
    gate = jax.nn.softmax(best.astype(jnp.float32), axis=-1).astype(h.dtype)
    n = (B * T) // PEER_CHUNK
    kk = PEER_HEADS * PEER_TOPK

    def chunk(args):
        hc, ec, gc = args
        act = jax.nn.gelu(jnp.einsum('ckd,cd->ck', u_tab[ec], hc))
        return jnp.einsum('ck,ckd->cd', gc * act, v_tab[ec])

    y = lax.map(chunk, (h.reshape(n, PEER_CHUNK, D), eidx.reshape(n, PEER_CHUNK, kk), gate.reshape(n, PEER_CHUNK, kk)))
    return y.reshape(B, T, D)


def setup_inputs(seed: int = 0) -> dict:
    key = jax.random.key(seed)
    keys = iter(jax.random.split(key, 32))

    def nrm(shape, std):
        return jax.random.normal(next(keys), shape, jnp.float32) * std

    D = D_MODEL
    return {
        'x': nrm((BATCH, SEQ, D), 1.0),
        'c': nrm((BATCH, D), 1.0),
        'ctx': nrm((BATCH, CTX_LEN, D), 1.0),
        'c_ctx': nrm((D,), 1.0),
        'mod_w': nrm((DEPTH, D, 6 * D), 0.5 * D ** -0.5),
        'mod_b': nrm((DEPTH, 6 * D), 0.02),
        'ln_g': 1.0 + nrm((DEPTH, 2, D), 0.02),
        'ln_b': nrm((DEPTH, 2, D), 0.02),
        'peer_wq': nrm((DEPTH, D, PEER_HEADS * PEER_DK), D ** -0.5),
        'peer_k1': nrm((DEPTH, PEER_NKEYS, PEER_DK // 2), (PEER_DK // 2) ** -0.5),
        'peer_k2': nrm((DEPTH, PEER_NKEYS, PEER_DK // 2), (PEER_DK // 2) ** -0.5),
        'peer_u': nrm((DEPTH, PEER_EXPERTS, D), D ** -0.5),
        'peer_v': nrm((DEPTH, PEER_EXPERTS, D), BETA),
        'gm_w_in': nrm((N_LAYERS_A, D, 2 * GM_DIM), D ** -0.5),
        'gm_ln_g': 1.0 + nrm((N_LAYERS_A, GM_DIM), 0.02),
        'gm_ln_b': nrm((N_LAYERS_A, GM_DIM), 0.02),
        'gm_ws': nrm((N_LAYERS_A, GM_GROUPS, GM_CHUNK, GM_CHUNK), GM_CHUNK ** -0.5),
        'gm_bs': 1.0 + nrm((N_LAYERS_A, GM_GROUPS, GM_CHUNK), 0.1),
        'gm_w_out': nrm((N_LAYERS_A, GM_DIM, D), BETA * GM_DIM ** -0.5),
        'mla_w_in': nrm((N_LAYERS_B, D, MLA_Q_RANK + MLA_KV_RANK + MLA_ROPE), D ** -0.5),
        'mla_q_norm': 1.0 + nrm((N_LAYERS_B, MLA_Q_RANK), 0.02),
        'mla_kv_norm': 1.0 + nrm((N_LAYERS_B, MLA_KV_RANK), 0.02),
        'mla_w_uq': nrm((N_LAYERS_B, MLA_Q_RANK, MLA_HEADS * (MLA_NOPE + MLA_ROPE)), MLA_Q_RANK ** -0.5),
        'mla_w_ukv': nrm((N_LAYERS_B, MLA_KV_RANK, MLA_HEADS * (MLA_NOPE + MLA_V)), MLA_KV_RANK ** -0.5),
        'mla_w_out': nrm((N_LAYERS_B, MLA_HEADS * MLA_V, D), BETA * (MLA_HEADS * MLA_V) ** -0.5),
        'da_w_in': nrm((N_LAYERS_C, D, 3 * D), D ** -0.5),
        'da_lambda': nrm((N_LAYERS_C, 4, DA_HEAD), 0.1),
        'da_subln': 1.0 + nrm((N_LAYERS_C, 2 * DA_HEAD), 0.02),
        'da_w_out': nrm((N_LAYERS_C, D, D), BETA * D ** -0.5),
    }


def reference(x, c, ctx, c_ctx, mod_w, mod_b, ln_g, ln_b,
              peer_wq, peer_k1, peer_k2, peer_u, peer_v,
              gm_w_in, gm_ln_g, gm_ln_b, gm_ws, gm_bs, gm_w_out,
              mla_w_in, mla_q_norm, mla_kv_norm, mla_w_uq, mla_w_ukv, mla_w_out,
              da_w_in, da_lambda, da_subln, da_w_out):
    n_tokens = x.shape[1]
    rope = axial_rope_tables(n_tokens)
    silu_c = jax.nn.silu(c)
    silu_cc = jax.nn.silu(c_ctx)
    xl, xc = x, ctx
    for i in range(DEPTH):
        kind = i % N_MIXERS
        j = i // N_MIXERS
        last = i == DEPTH - 1
        mod_l = silu_c @ mod_w[i] + mod_b[i]
        mod_c = silu_cc @ mod_w[i] + mod_b[i]
        sh1, sc1, g1, sh2, sc2, g2 = jnp.split(mod_l[:, None, :], 6, axis=-1)
        csh1, csc1, cg1, csh2, csc2, cg2 = jnp.split(mod_c, 6, axis=-1)
        hl = _modulate(xl, sh1, sc1)
        need_ctx_in = (not last) or kind != 0
        hc = _modulate(xc, csh1, csc1) if need_ctx_in else None
        if kind == 0:
            yl, yc = gmlp_mixer(hl, hc, gm_w_in[j], gm_ln_g[j], gm_ln_b[j], gm_ws[j], gm_bs[j], gm_w_out[j])
        elif kind == 1:
            yl, yc = mla_mixer(hl, hc, rope, mla_w_in[j], mla_q_norm[j], mla_kv_norm[j],
                               mla_w_uq[j], mla_w_ukv[j], mla_w_out[j], not last)
        else:
            lam_init = 0.8 - 0.6 * math.exp(-0.3 * i)
            yl, yc = diff_attn_mixer(hl, hc, rope, da_w_in[j], da_lambda[j], da_subln[j],
                                     da_w_out[j], lam_init, not last)
        xl = _post_norm(xl, g1 * yl, ln_g[i, 0], ln_b[i, 0])
        yl = peer_ffn(_modulate(xl, sh2, sc2), peer_wq[i], peer_k1[i], peer_k2[i], peer_u[i], peer_v[i])
        xl = _post_norm(xl, g2 * yl, ln_g[i, 1], ln_b[i, 1])
        if not last:
            xc = _post_norm(xc, cg1 * yc, ln_g[i, 0], ln_b[i, 0])
            yc = peer_ffn(_modulate(xc, csh2, csc2), peer_wq[i], peer_k1[i], peer_k2[i], peer_u[i], peer_v[i])
            xc = _post_norm(xc, cg2 * yc, ln_g[i, 1], ln_b[i, 1])
    return xl
```

```python
import math
import os
CUT = int(os.environ.get('MK_CUT', '99'))
from contextlib import ExitStack

import numpy as np
import ml_dtypes
import concourse.bass as bass
import concourse.mybir as mybir
from concourse.bass_utils import run_bass_kernel_spmd

F32 = mybir.dt.float32
BF16 = mybir.dt.bfloat16
AF = mybir.ActivationFunctionType
ALU = mybir.AluOpType

D = 2048
KC = 16
DEPTH = 4
ALPHA = (2.0 * DEPTH) ** 0.25
LN_EPS = 1e-6
GRID_W = 64
MLA_SCALE = (128 + 64) ** -0.5
DA_SCALE = 64 ** -0.5
NEG = -1.0e30


class Buf:
    __slots__ = ("t", "w", "r")

    def __init__(self, t=None):
        self.t = t
        self.w = None
        self.r = {}

    def __getitem__(self, idx):
        return self.t[idx]


class DBuf:
    def __init__(self, ap):
        self.ap = ap
        self.bufs = {}

    def b(self, key=0):
        if key not in self.bufs:
            self.bufs[key] = Buf()
        return self.bufs[key]

    def all(self):
        return list(self.bufs.values())


class Prog:
    NDMA = 24

    def __init__(self, nc, es):
        self.nc = nc
        self.es = es
        self.cur = es
        self.eng = {'pe': nc.tensor, 'dve': nc.vector, 'act': nc.scalar, 'pool': nc.gpsimd, 'sp': nc.sync}
        self.sems = []
        self.semval = []
        self.engsem = {e: self.new_sem("s_" + e) for e in self.eng}
        self.waited = {e: {} for e in self.eng}
        self.dma_sems = [self.new_sem("d%d" % i) for i in range(self.NDMA)]
        self.dma_rr = 0
        self.nb = 0

    def new_sem(self, name):
        s = self.es.enter_context(self.nc.semaphore(name))
        self.sems.append(s)
        self.semval.append(0)
        return len(self.sems) - 1

    def sb(self, shape, dt):
        self.nb += 1
        t = self.cur.enter_context(self.nc.sbuf_tensor("sb%d" % self.nb, list(shape), dt))
        return Buf(t)

    def ps(self, shape, dt):
        self.nb += 1
        t = self.cur.enter_context(self.nc.psum_tensor("ps%d" % self.nb, list(shape), dt))
        return Buf(t)

    def _deps(self, reads, writes):
        need = {}
        for b in reads:
            if b.w is not None and need.get(b.w[0], 0) < b.w[1]:
                need[b.w[0]] = b.w[1]
        for b in writes:
            if b.w is not None and need.get(b.w[0], 0) < b.w[1]:
                need[b.w[0]] = b.w[1]
            for si, v in b.r.items():
                if need.get(si, 0) < v:
                    need[si] = v
        return need

    def _wait(self, e, need, skip_self=False):
        own = self.engsem.get(e)
        w = self.waited[e]
        for si, v in need.items():
            if skip_self and si == own:
                continue
            if w.get(si, 0) < v:
                self.eng[e].wait_ge(self.sems[si], v)
                w[si] = v

    def _mark(self, ev, reads, writes):
        si, v = ev
        for b in reads:
            if b.r.get(si, 0) < v:
                b.r[si] = v
        for b in writes:
            b.w = ev
            b.r = {}

    def op(self, e, fn, reads=(), writes=()):
        self._wait(e, self._deps(reads, writes), skip_self=(e == 'pe'))
        ins = fn()
        si = self.engsem[e]
        self.semval[si] += 1
        ins.then_inc(self.sems[si], 1)
        ev = (si, self.semval[si])
        self._mark(ev, reads, writes)
        return ev

    def mm(self, fns, reads=(), writes=()):
        self._wait('pe', self._deps(reads, writes), skip_self=True)
        ins = None
        for fn in fns:
            ins = fn()
        si = self.engsem['pe']
        self.semval[si] += 1
        ins.then_inc(self.sems[si], 1)
        ev = (si, self.semval[si])
        self._mark(ev, reads, writes)
        return ev

    def grp(self, e, fns, reads=(), writes=()):
        self._wait(e, self._deps(reads, writes), skip_self=(e == 'pe'))
        ins = None
        for fn in fns:
            ins = fn()
        si = self.engsem[e]
        self.semval[si] += 1
        ins.then_inc(self.sems[si], 1)
        ev = (si, self.semval[si])
        self._mark(ev, reads, writes)
        return ev

    def dma(self, out, in_, reads=(), writes=(), q='sp', **kw):
        si = self.dma_sems[self.dma_rr % self.NDMA]
        self.dma_rr += 1
        need = self._deps(reads, writes)
        if self.semval[si] > 0 and need.get(si, 0) < self.semval[si]:
            need[si] = self.semval[si]
        self._wait(q, need)
        ins = self.eng[q].dma_start(out=out, in_=in_, **kw)
        self.semval[si] += 16
        ins.then_inc(self.sems[si], 16)
        ev = (si, self.semval[si])
        self._mark(ev, reads, writes)
        return ev

    def barrier(self):
        need = {si: v for si, v in enumerate(self.semval) if v > 0}
        for e in self.eng:
            self._wait(e, need)


def V(P, fn, r, w):
    return P.op('dve', fn, r, w)


def A(P, fn, r, w):
    return P.op('act', fn, r, w)


def G(P, fn, r, w):
    return P.op('pool', fn, r, w)


class Builder:
    def __init__(self, TL, phase, layers=(0, 1, 2, 3), skip=()):
        self.skip = set(skip)
        self.TL = TL
        self.NTI = TL + 1
        self.NTOK = self.NTI * 128
        self.phase = phase
        self.layers = layers
        self.nc = bass.Bass("TRN2", target_bir_lowering=False)
        self.inputs = {}
        self.outputs = {}
        self.mod_done = set()

    def din(self, name, shape, dt=F32):
        t = self.nc.dram_tensor(name, list(shape), dt, kind="ExternalInput")
        self.inputs[name] = (tuple(shape), dt)
        return DBuf(t.ap())

    def dout(self, name, shape, dt=F32):
        t = self.nc.dram_tensor(name, list(shape), dt, kind="ExternalOutput")
        self.outputs[name] = (tuple(shape), dt)
        return DBuf(t.ap())

    def dscr(self, name, shape, dt=F32):
        t = self.nc.dram_tensor(name, list(shape), dt)
        d = DBuf(t.ap())
        d.th = t
        return d

    def xphase(self, name, shape, dt, prod, cons):
        ph = self.phase
        if ph == 'all' or prod == cons:
            return self.dscr(name, shape, dt)
        if ph == prod:
            return self.dout(name, shape, dt)
        if ph == cons:
            return self.din(name, shape, dt)
        return None

    def consts(self):
        P, nc = self.P, self.nc
        self.identf = P.sb([128, 128], F32)
        self.ident = P.sb([128, 128], BF16)
        self.onesb = P.sb([128, 128], BF16)
        self.onesf = P.sb([128, 128], F32)
        self.epsb = P.sb([128, 1], F32)
        G(P, lambda: nc.gpsimd.memset(self.identf[:], 1.0), [], [self.identf])
        G(P, lambda: nc.gpsimd.affine_select(out=self.identf[:], in_=self.identf[:], pattern=[[-1, 128]],
                                             compare_op=ALU.is_equal, fill=0.0, base=0, channel_multiplier=1),
          [self.identf], [self.identf])
        A(P, lambda: nc.scalar.copy(out=self.ident[:], in_=self.identf[:]), [self.identf], [self.ident])
        G(P, lambda: nc.gpsimd.memset(self.onesb[:], 1.0), [], [self.onesb])
        G(P, lambda: nc.gpsimd.memset(self.onesf[:], 1.0), [], [self.onesf])
        G(P, lambda: nc.gpsimd.memset(self.epsb[:], LN_EPS), [], [self.epsb])

    def bcast_tile(self, dst, src_dbuf, row_ap, key=0):
        self.P.dma(dst[:], row_ap.partition_broadcast(128), reads=[src_dbuf.b(key)], writes=[dst])

    def load_w_bf16(self, wb, w_ap, wbuf, K, N, stages, c0=0):
        P, nc = self.P, self.nc
        for kc in range(K // 128):
            st = stages[kc % len(stages)]
            P.dma(st[:, 0:N], w_ap[kc * 128:(kc + 1) * 128, c0:c0 + N], reads=[wbuf], writes=[st])
            if kc % 2 == 0:
                V(P, lambda st=st, kc=kc: nc.vector.tensor_copy(out=wb[:, kc, :], in_=st[:, 0:N]), [st], [wb])
            else:
                A(P, lambda st=st, kc=kc: nc.scalar.copy(out=wb[:, kc, :], in_=st[:, 0:N]), [st], [wb])

    def ln_stats(self, x, st, mv, rs, n=2048):
        P, nc = self.P, self.nc
        nchunk = n // 512
        for c in range(nchunk):
            V(P, lambda c=c: nc.vector.bn_stats(out=st[:, c, :], in_=x[:, c * 512:(c + 1) * 512]), [x], [st])
        V(P, lambda: nc.vector.bn_aggr(out=mv[:], in_=st[:, 0:nchunk, :]), [st], [mv])
        A(P, lambda: nc.scalar.activation(out=rs[:], in_=mv[:, 1:2], func=AF.Sqrt, bias=self.epsb[:, 0:1], scale=1.0),
          [mv, self.epsb], [rs])
        V(P, lambda: nc.vector.reciprocal(out=rs[:], in_=rs[:]), [rs], [rs])

    def tiles_of_layer(self, li):
        return list(range(self.TL)) if li == DEPTH - 1 else list(range(self.NTI))

    def mod_pass(self, li):
        if li in self.mod_done:
            return
        self.mod_done.add(li)
        P, nc = self.P, self.nc
        with ExitStack() as pes:
            P.cur = pes
            cT = P.sb([128, KC, 2], F32)
            sT = P.sb([128, KC, 2], F32)
            wst = [P.sb([128, KC, 512], F32) for _ in range(2)]
            bia = [P.sb([2, 512], F32) for _ in range(2)]
            res = [P.sb([2, 512], F32) for _ in range(2)]
            pm = [P.ps([2, 512], F32) for _ in range(2)]
            P.dma(cT[:], self.cT.ap[:, :, :], reads=[self.cT.b()], writes=[cT])
            A(P, lambda: nc.scalar.activation(out=sT[:], in_=cT[:], func=AF.Silu), [cT], [sT])
            wv = self.mod_w.ap[self.midx[li]].rearrange("(kc p) n -> p kc n", p=128)
            for nb in range(24):
                w = wst[nb % 2]
                bi = bia[nb % 2]
                rr = res[nb % 2]
                pp = pm[nb % 2]
                P.dma(w[:], wv[:, :, nb * 512:(nb + 1) * 512], reads=[self.mod_w.b()], writes=[w])
                P.dma(bi[:], self.mod_b.ap[li, nb * 512:(nb + 1) * 512].partition_broadcast(2),
                      reads=[self.mod_b.b()], writes=[bi])
                P.mm([lambda kc=kc, w=w, pp=pp: nc.tensor.matmul(pp[:], lhsT=sT[:, kc, :], rhs=w[:, kc, :],
                                                                   start=(kc == 0), stop=(kc == KC - 1))
                      for kc in range(KC)], reads=[sT, w], writes=[pp])
                seg = nb // 4
                add1 = 1.0 if seg in (1, 4) else 0.0
                V(P, lambda pp=pp, bi=bi, rr=rr, add1=add1: nc.vector.scalar_tensor_tensor(
                    out=rr[:], in0=pp[:], scalar=add1, in1=bi[:], op0=ALU.add, op1=ALU.add), [pp, bi], [rr])
                P.dma(self.MOD.ap[li, :, nb * 512:(nb + 1) * 512], rr[:], reads=[rr], writes=[self.MOD.b(li)])
            P.barrier()
        P.cur = P.es

    def mod_row(self, li, m, seg):
        return self.MOD.ap[li, m, seg * 2048:(seg + 1) * 2048]

    def ln_mod_pass(self, li, seg_shift, seg_scale, HT):
        P, nc = self.P, self.nc
        tiles = self.tiles_of_layer(li)
        with ExitStack() as pes:
            P.cur = pes
            SC = [P.sb([128, D], F32) for _ in range(2)]
            SH = [P.sb([128, D], F32) for _ in range(2)]
            for m in range(2):
                self.bcast_tile(SC[m], self.MOD, self.mod_row(li, m, seg_scale), li)
                self.bcast_tile(SH[m], self.MOD, self.mod_row(li, m, seg_shift), li)
            xs = [P.sb([128, D], F32) for _ in range(2)]
            wk = [P.sb([128, D], F32) for _ in range(2)]
            hb = [P.sb([128, D], BF16) for _ in range(2)]
            hT = [P.sb([128, KC, 128], BF16) for _ in range(2)]
            st = [P.sb([128, 4, 6], F32) for _ in range(2)]
            mv = [P.sb([128, 2], F32) for _ in range(2)]
            rs = [P.sb([128, 1], F32) for _ in range(2)]
            pT = [P.ps([128, D], BF16) for _ in range(2)]
            for i, t in enumerate(tiles):
                k = i % 2
                m = 1 if t == self.TL else 0
                x = xs[k]
                P.dma(x[:], self.X.ap[t], reads=[self.X.b(t)], writes=[x])
                self.ln_stats(x, st[k], mv[k], rs[k])
                V(P, lambda x=x, k=k: nc.vector.tensor_scalar(out=wk[k][:], in0=x[:], scalar1=mv[k][:, 0:1],
                                                              scalar2=rs[k][:, 0:1], op0=ALU.subtract, op1=ALU.mult),
                  [x, mv[k], rs[k]], [wk[k]])
                V(P, lambda k=k, m=m: nc.vector.tensor_tensor(out=wk[k][:], in0=wk[k][:], in1=SC[m][:], op=ALU.mult),
                  [wk[k], SC[m]], [wk[k]])
                V(P, lambda k=k, m=m: nc.vector.tensor_tensor(out=hb[k][:], in0=wk[k][:], in1=SH[m][:], op=ALU.add),
                  [wk[k], SH[m]], [hb[k]])
                P.mm([lambda c=c, k=k: nc.tensor.transpose(out=pT[k][:, c * 128:(c + 1) * 128],
                                                           in_=hb[k][:, c * 128:(c + 1) * 128], identity=self.ident[:])
                      for c in range(KC)], reads=[hb[k], self.ident], writes=[pT[k]])
                A(P, lambda k=k: nc.scalar.copy(out=hT[k][:].rearrange("p a b -> p (a b)"), in_=pT[k][:]), [pT[k]], [hT[k]])
                P.dma(HT.ap[t], hT[k][:].rearrange("p a b -> p (a b)"), reads=[hT[k]], writes=[HT.b(t)])
            P.barrier()
        P.cur = P.es

    def postnorm_alloc(self, li, sub, seg_gate):
        P = self.P
        pn = {}
        pn['G'] = [P.sb([128, D], F32) for _ in range(2)]
        pn['LG'] = P.sb([128, D], F32)
        pn['LB'] = P.sb([128, D], F32)
        for m in range(2):
            self.bcast_tile(pn['G'][m], self.MOD, self.mod_row(li, m, seg_gate), li)
        self.bcast_tile(pn['LG'], self.ln_g, self.ln_g.ap[li, sub, :])
        self.bcast_tile(pn['LB'], self.ln_b, self.ln_b.ap[li, sub, :])
        pn['x'] = [P.sb([128, D], F32) for _ in range(2)]
        pn['z'] = [P.sb([128, D], F32) for _ in range(2)]
        pn['st'] = [P.sb([128, 4, 6], F32) for _ in range(2)]
        pn['mv'] = [P.sb([128, 2], F32) for _ in range(2)]
        pn['rs'] = [P.sb([128, 1], F32) for _ in range(2)]
        pn['i'] = 0
        return pn

    def postnorm_tile(self, pn, t, ybuf, yap):
        P, nc = self.P, self.nc
        k = pn['i'] % 2
        pn['i'] += 1
        m = 1 if t == self.TL else 0
        x, z, st, mv, rs = pn['x'][k], pn['z'][k], pn['st'][k], pn['mv'][k], pn['rs'][k]
        P.dma(x[:], self.X.ap[t], reads=[self.X.b(t)], writes=[x])
        V(P, lambda: nc.vector.tensor_tensor(out=z[:], in0=yap, in1=pn['G'][m][:], op=ALU.mult), [ybuf, pn['G'][m]], [z])
        V(P, lambda: nc.vector.scalar_tensor_tensor(out=z[:], in0=x[:], scalar=ALPHA, in1=z[:], op0=ALU.mult, op1=ALU.add),
          [x, z], [z])
        self.ln_stats(z, st, mv, rs)
        V(P, lambda: nc.vector.tensor_scalar(out=z[:], in0=z[:], scalar1=mv[:, 0:1], scalar2=rs[:, 0:1],
                                             op0=ALU.subtract, op1=ALU.mult), [z, mv, rs], [z])
        V(P, lambda: nc.vector.tensor_tensor(out=z[:], in0=z[:], in1=pn['LG'][:], op=ALU.mult), [z, pn['LG']], [z])
        V(P, lambda: nc.vector.tensor_tensor(out=x[:], in0=z[:], in1=pn['LB'][:], op=ALU.add), [z, pn['LB']], [x])
        P.dma(self.X.ap[t], x[:], reads=[x], writes=[self.X.b(t)])

    def outproj_pass(self, li, w_ap, wbuf, AT, at_view):
        P, nc = self.P, self.nc
        tiles = self.tiles_of_layer(li)
        with ExitStack() as pes:
            P.cur = pes
            Wo = P.sb([128, KC, D], BF16)
            stg = [P.sb([128, D], F32) for _ in range(2)]
            self.load_w_bf16(Wo, w_ap, wbuf, D, D, stg)
            pn = self.postnorm_alloc(li, 0, 2)
            aT = [P.sb([128, KC, 128], BF16) for _ in range(2)]
            py = P.ps([128, D], F32)
            for i, t in enumerate(tiles):
                a = aT[i % 2]
                P.dma(a[:], at_view(t), reads=[AT.b(t)], writes=[a])
                P.mm([lambda g=g, nb=nb, a=a: nc.tensor.matmul(py[:, nb * 512:(nb + 1) * 512], lhsT=a[:, g, :],
                                                              rhs=Wo[:, g, nb * 512:(nb + 1) * 512],
                                                              start=(g == 0), stop=(g == KC - 1))
                      for nb in range(4) for g in range(KC)], reads=[a, Wo], writes=[py])
                self.postnorm_tile(pn, t, py, py[:])
            P.barrier()
        P.cur = P.es

    def gmlp(self, li, j):
        P, nc = self.P, self.nc
        tiles = self.tiles_of_layer(li)
        self.ln_mod_pass(li, 0, 1, self.HT)
        with ExitStack() as pes:
            P.cur = pes
            Wv = P.sb([128, KC, D], BF16)
            stg = [P.sb([128, D], F32) for _ in range(2)]
            self.load_w_bf16(Wv, self.gm_w_in.ap[j], self.gm_w_in.b(), D, D, stg, c0=D)
            GG = P.sb([128, D], F32)
            GB = P.sb([128, D], F32)
            self.bcast_tile(GG, self.gm_ln_g, self.gm_ln_g.ap[j, :])
            self.bcast_tile(GB, self.gm_ln_b, self.gm_ln_b.ap[j, :])
            hT = [P.sb([128, KC, 128], BF16) for _ in range(2)]
            v = [P.sb([128, D], F32) for _ in range(2)]
            vb = [P.sb([128, D], BF16) for _ in range(2)]
            st = [P.sb([128, 4, 6], F32) for _ in range(2)]
            mv = [P.sb([128, 2], F32) for _ in range(2)]
            rs = [P.sb([128, 1], F32) for _ in range(2)]
            pv = P.ps([128, D], F32)
            for i, t in enumerate(tiles):
                k = i % 2
                h = hT[k]
                P.dma(h[:].rearrange("p a b -> p (a b)"), self.HT.ap[t], reads=[self.HT.b(t)], writes=[h])
                P.mm([lambda kc=kc, nb=nb, h=h: nc.tensor.matmul(pv[:, nb * 512:(nb + 1) * 512], lhsT=h[:, kc, :],
                                                                 rhs=Wv[:, kc, nb * 512:(nb + 1) * 512],
                                                                 start=(kc == 0), stop=(kc == KC - 1))
                      for nb in range(4) for kc in range(KC)], reads=[h, Wv], writes=[pv])
                A(P, lambda k=k: nc.scalar.activation(out=v[k][:], in_=pv[:], func=AF.Gelu_apprx_tanh), [pv], [v[k]])
                self.ln_stats(v[k], st[k], mv[k], rs[k])
                V(P, lambda k=k: nc.vector.tensor_scalar(out=v[k][:], in0=v[k][:], scalar1=mv[k][:, 0:1],
                                                         scalar2=rs[k][:, 0:1], op0=ALU.subtract, op1=ALU.mult),
                  [v[k], mv[k], rs[k]], [v[k]])
                V(P, lambda k=k: nc.vector.tensor_tensor(out=v[k][:], in0=v[k][:], in1=GG[:], op=ALU.mult), [v[k], GG], [v[k]])
                V(P, lambda k=k: nc.vector.tensor_tensor(out=vb[k][:], in0=v[k][:], in1=GB[:], op=ALU.add), [v[k], GB], [vb[k]])
                P.dma(self.VN.ap[t], vb[k][:], reads=[vb[k]], writes=[self.VN.b(t)])
            P.barrier()
        with ExitStack() as pes:
            P.cur = pes
            Wu = P.sb([128, KC, D], BF16)
            stg = [P.sb([128, D], F32) for _ in range(2)]
            self.load_w_bf16(Wu, self.gm_w_in.ap[j], self.gm_w_in.b(), D, D, stg, c0=0)
            wsf = P.sb([128, KC, 128], F32)
            wsT = P.sb([128, KC, 128], BF16)
            P.dma(wsf[:], self.gm_wsT.ap[j].rearrange("g q p -> q g p"), reads=[self.gm_wsT.b()], writes=[wsf])
            G(P, lambda: nc.gpsimd.tensor_copy(out=wsT[:], in_=wsf[:]), [wsf], [wsT])
            bsT = P.sb([128, D], F32)
            self.bcast_tile(bsT, self.gm_bs, self.gm_bs.ap[j].rearrange("g p -> (g p)"))
            hT = [P.sb([128, KC, 128], BF16) for _ in range(2)]
            vb = [P.sb([128, D], BF16) for _ in range(2)]
            uT = [P.sb([128, D], F32) for _ in range(2)]
            t1 = [P.sb([128, D], F32) for _ in range(2)]
            mT = [P.sb([128, D], BF16) for _ in range(2)]
            pu = P.ps([128, D], F32)
            psv = P.ps([128, D], F32)
            for i, t in enumerate(tiles):
                k = i % 2
                h = hT[k]
                P.dma(h[:].rearrange("p a b -> p (a b)"), self.HT.ap[t], reads=[self.HT.b(t)], writes=[h])
                P.dma(vb[k][:], self.VN.ap[t], reads=[self.VN.b(t)], writes=[vb[k]])
                P.mm([lambda kc=kc, g=g, h=h: nc.tensor.matmul(pu[:, g * 128:(g + 1) * 128],
                                                               lhsT=Wu[:, kc, g * 128:(g + 1) * 128], rhs=h[:, kc, :],
                                                               start=(kc == 0), stop=(kc == KC - 1))
                      for g in range(KC) for kc in range(KC)], reads=[h, Wu], writes=[pu])
                A(P, lambda k=k: nc.scalar.activation(out=uT[k][:], in_=pu[:], func=AF.Gelu_apprx_tanh), [pu], [uT[k]])
                P.mm([lambda g=g, k=k: nc.tensor.matmul(psv[:, g * 128:(g + 1) * 128],
                                                        lhsT=vb[k][:, g * 128:(g + 1) * 128], rhs=wsT[:, g, :],
                                                        start=True, stop=True)
                      for g in range(KC)], reads=[vb[k], wsT], writes=[psv])
                V(P, lambda k=k: nc.vector.tensor_tensor(out=t1[k][:], in0=psv[:], in1=bsT[:], op=ALU.add), [psv, bsT], [t1[k]])
                V(P, lambda k=k: nc.vector.tensor_tensor(out=mT[k][:], in0=t1[k][:], in1=uT[k][:], op=ALU.mult),
                  [t1[k], uT[k]], [mT[k]])
                P.dma(self.MT.ap[t], mT[k][:], reads=[mT[k]], writes=[self.MT.b(t)])
            P.barrier()
        self.outproj_pass(li, self.gm_w_out.ap[j], self.gm_w_out.b(), self.MT, lambda t: self.MT.ap[t].rearrange("p (a b) -> p a b", b=128))

    def peer(self, li):
        P, nc = self.P, self.nc
        tiles = self.tiles_of_layer(li)
        self.ln_mod_pass(li, 3, 4, self.HT)
        with ExitStack() as pes:
            P.cur = pes
            Wq = P.sb([128, KC, D], BF16)
            stg = [P.sb([128, D], F32) for _ in range(2)]
            self.load_w_bf16(Wq, self.peer_wq.ap[self.lidx[li]], self.peer_wq.b(), D, D, stg)
            kTf = P.sb([128, 2, 128], F32)
            kT = P.sb([128, 2, 128], BF16)
            P.dma(kTf[:, 0, :], self.peer_k1T.ap[self.lidx[li]], reads=[self.peer_k1T.b()], writes=[kTf])
            P.dma(kTf[:, 1, :], self.peer_k2T.ap[self.lidx[li]], reads=[self.peer_k2T.b()], writes=[kTf])
            G(P, lambda: nc.gpsimd.tensor_copy(out=kT[:], in_=kTf[:]), [kTf], [kT])
            hT = [P.sb([128, KC, 128], BF16) for _ in range(2)]
            qT = P.sb([128, KC, 128], BF16)
            S = P.sb([128, 16, 128], F32)
            TMP = P.sb([128, 128], F32)
            V16 = P.sb([128, 16, 16], F32)
            C = P.sb([128, 8, 256], F32)
            T2 = P.sb([128, 256], F32)
            T3 = P.sb([128, 256], F32)
            B24 = P.sb([128, 8, 24], F32)
            Dm = P.sb([128, 8, 16], F32)
            Z = P.sb([128, 8], F32)
            TAU = P.sb([128, 8], F32)
            TH = P.sb([128, 8], F32)
            W8 = P.sb([128, 8], F32)
            E = [P.sb([128, 8, 256], F32) for _ in range(2)]
            DG = [P.sb([128, 8, 128], BF16) for _ in range(2)]
            pq = P.ps([128, D], F32)
            pS = P.ps([128, D], F32)
            for i, t in enumerate(tiles):
                k = i % 2
                h = hT[k]
                P.dma(h[:].rearrange("p a b -> p (a b)"), self.HT.ap[t], reads=[self.HT.b(t)], writes=[h])
                P.mm([lambda kc=kc, n=n, h=h: nc.tensor.matmul(pq[:, n * 128:(n + 1) * 128],
                                                               lhsT=Wq[:, kc, n * 128:(n + 1) * 128], rhs=h[:, kc, :],
                                                               start=(kc == 0), stop=(kc == KC - 1))
                      for n in range(16) for kc in range(KC)], reads=[h, Wq], writes=[pq])
                A(P, lambda: nc.scalar.copy(out=qT[:].rearrange("p a b -> p (a b)"), in_=pq[:]), [pq], [qT])
                P.mm([lambda n=n: nc.tensor.matmul(pS[:, n * 128:(n + 1) * 128], lhsT=qT[:, n, :], rhs=kT[:, n % 2, :],
                                                   start=True, stop=True) for n in range(16)],
                     reads=[qT, kT], writes=[pS])
                A(P, lambda: nc.scalar.copy(out=S[:].rearrange("p a b -> p (a b)"), in_=pS[:]), [pS], [S])
                for n in range(16):
                    V(P, lambda n=n: nc.vector.max(out=V16[:, n, 0:8], in_=S[:, n, :]), [S], [V16])
                    V(P, lambda n=n: nc.vector.match_replace(out=TMP[:], in_to_replace=V16[:, n, 0:8], in_values=S[:, n, :],
                                                             imm_value=NEG), [S, V16], [TMP])
                    V(P, lambda n=n: nc.vector.max(out=V16[:, n, 8:16], in_=TMP[:]), [TMP], [V16])
                Vv = V16[:].rearrange("p (h two) a -> p h two a", two=2)
                in0 = Vv[:, :, 0, :].unsqueeze(3).to_broadcast([128, 8, 16, 16])
                in1 = Vv[:, :, 1, :].unsqueeze(2).to_broadcast([128, 8, 16, 16])
                V(P, lambda: nc.vector.tensor_tensor(out=C[:].rearrange("p h (a b) -> p h a b", b=16), in0=in0, in1=in1,
                                                     op=ALU.add), [V16], [C])
                for hh in range(8):
                    V(P, lambda hh=hh: nc.vector.max(out=B24[:, hh, 0:8], in_=C[:, hh, :]), [C], [B24])
                    V(P, lambda hh=hh: nc.vector.match_replace(out=T2[:], in_to_replace=B24[:, hh, 0:8], in_values=C[:, hh, :],
                                                               imm_value=NEG), [C, B24], [T2])
                    V(P, lambda hh=hh: nc.vector.max(out=B24[:, hh, 8:16], in_=T2[:]), [T2], [B24])
                    V(P, lambda hh=hh: nc.vector.match_replace(out=T3[:], in_to_replace=B24[:, hh, 8:16], in_values=T2[:],
                                                               imm_value=NEG), [T2, B24], [T3])
                    V(P, lambda hh=hh: nc.vector.max(out=B24[:, hh, 16:24], in_=T3[:]), [T3], [B24])
                V(P, lambda: nc.vector.tensor_tensor(out=Dm[:], in0=B24[:, :, 0:16],
                                                     in1=B24[:, :, 0:1].to_broadcast([128, 8, 16]), op=ALU.subtract),
                  [B24], [Dm])
                A(P, lambda: nc.scalar.activation(out=Dm[:], in_=Dm[:], func=AF.Exp), [Dm], [Dm])
                V(P, lambda: nc.vector.tensor_reduce(out=Z[:], in_=Dm[:], axis=mybir.AxisListType.X, op=ALU.add), [Dm], [Z])
                V(P, lambda: nc.vector.tensor_tensor(out=TAU[:], in0=B24[:, :, 15], in1=B24[:, :, 16], op=ALU.add), [B24], [TAU])
                V(P, lambda: nc.vector.tensor_scalar(out=TH[:], in0=TAU[:], scalar1=-0.25, scalar2=None, op0=ALU.mult), [TAU], [TH])
                V(P, lambda: nc.vector.scalar_tensor_tensor(out=W8[:], in0=TAU[:], scalar=0.5, in1=B24[:, :, 0],
                                                            op0=ALU.mult, op1=ALU.subtract), [TAU, B24], [W8])
                A(P, lambda: nc.scalar.activation(out=W8[:], in_=W8[:], func=AF.Exp), [W8], [W8])
                V(P, lambda: nc.vector.reciprocal(out=Z[:], in_=Z[:]), [Z], [Z])
                V(P, lambda: nc.vector.tensor_tensor(out=W8[:], in0=W8[:], in1=Z[:], op=ALU.mult), [W8, Z], [W8])
                e = E[k]
                V(P, lambda e=e: nc.vector.tensor_tensor(out=e[:], in0=S[:].rearrange("p (h two) i -> p h (two i)", two=2),
                                                         in1=TH[:].unsqueeze(2).to_broadcast([128, 8, 256]), op=ALU.add),
                  [S, TH], [e])
                A(P, lambda e=e: nc.scalar.activation(out=e[:], in_=e[:], func=AF.Exp), [e], [e])
                dg = DG[k]
                for hh in range(8):
                    V(P, lambda hh=hh, dg=dg: nc.vector.tensor_scalar(out=dg[:, hh, :], in0=self.identf[:],
                                                                      scalar1=W8[:, hh:hh + 1], scalar2=None, op0=ALU.mult),
                      [self.identf, W8], [dg])
                P.dma(self.EE.ap[t], e[:].rearrange("p a b -> p (a b)"), reads=[e], writes=[self.EE.b(t)])
                P.dma(self.DGS.ap[t], dg[:].rearrange("p a b -> p (a b)"), reads=[dg], writes=[self.DGS.b(t)])
            P.barrier()
        GS = 4
        EG = 4
        NEC = 128
        NEG_ = NEC // EG
        groups = [tiles[i:i + GS] for i in range(0, len(tiles), GS)]
        lw = self.lidx[li]
        with ExitStack() as pes:
            P.cur = pes
            yacc = [P.sb([128, D], F32) for _ in range(GS)]
            hAll = P.sb([128, KC, GS * 128], BF16)
            Eg = [P.sb([128, 8, 256], F32) for _ in range(GS)]
            Dg = [P.sb([128, 8, 128], BF16) for _ in range(GS)]
            ust = P.sb([128, KC, 128], F32)
            vst = P.sb([128, D], F32)
            ub = [P.sb([128, KC, 128], BF16) for _ in range(2 * EG)]
            vbb = [P.sb([128, D], BF16) for _ in range(2 * EG)]
            AbB = [P.sb([128, GS * 128], BF16) for _ in range(2 * EG)]
            Pp = [P.sb([128, 8, 128], F32) for _ in range(2)]
            PpB = [Buf(Pp[0].t), Buf(Pp[1].t)]
            Gp = [P.sb([128, 8, 128], BF16) for _ in range(2)]
            GA = [P.sb([128, 128], BF16) for _ in range(2)]
            pactB = [P.ps([128, 512], F32) for _ in range(2)]
            pgt = [P.ps([128, 128], F32) for _ in range(2)]
            py = P.ps([128, D], F32)
            a0n = [0]

            def load_eg(eg, first):
                for ci in range(EG):
                    ec = eg * EG + ci
                    s = (eg % 2) * EG + ci
                    if first:
                        P.dma(ust[:], self.peer_uL.ap[lw, ec], reads=[self.peer_uL.b()], writes=[ust])
                        A(P, lambda s=s: nc.scalar.copy(out=ub[s][:], in_=ust[:]), [ust], [ub[s]])
                        P.dma(self.UB.ap[ec], ub[s][:].rearrange("p a b -> p (a b)"), reads=[ub[s]], writes=[self.UB.b(ec)])
                        P.dma(vst[:], self.peer_v.ap[lw, ec * 128:(ec + 1) * 128, :], reads=[self.peer_v.b()], writes=[vst])
                        A(P, lambda s=s: nc.scalar.copy(out=vbb[s][:], in_=vst[:]), [vst], [vbb[s]])
                        P.dma(self.VB.ap[ec], vbb[s][:], reads=[vbb[s]], writes=[self.VB.b(ec)])
                    else:
                        P.dma(ub[s][:].rearrange("p a b -> p (a b)"), self.UB.ap[ec], reads=[self.UB.b(ec)], writes=[ub[s]])
                        P.dma(vbb[s][:], self.VB.ap[ec], reads=[self.VB.b(ec)], writes=[vbb[s]])

            def a0(eg, ci, ng):
                s = (eg % 2) * EG + ci
                k = a0n[0] % 2
                a0n[0] += 1
                n = ng * 128
                P.mm([lambda kc=kc: nc.tensor.matmul(pactB[k][:, 0:n], lhsT=ub[s][:, kc, :], rhs=hAll[:, kc, 0:n],
                                                     start=(kc == 0), stop=(kc == KC - 1))
                      for kc in range(KC)], reads=[ub[s], hAll], writes=[pactB[k]])
                A(P, lambda: nc.scalar.activation(out=AbB[s][:, 0:n], in_=pactB[k][:, 0:n], func=AF.Gelu_apprx_tanh),
                  [pactB[k]], [AbB[s]])

            def stage_a(itm, k):
                eg, gi, ci = itm
                ec = eg * EG + ci
                e = Eg[gi]
                NH = 5
                V(P, lambda: nc.vector.tensor_tensor(
                    out=Pp[k][:, 0:NH, :], in0=e[:, 0:NH, 128:256], in1=e[:, 0:NH, ec:ec + 1].to_broadcast([128, NH, 128]),
                    op=ALU.mult), [e], [Pp[k]])
                P.grp('act', [lambda hh=hh: nc.scalar.activation(out=PpB[k][:, hh, :], in_=e[:, hh, 128:256], func=AF.Copy,
                                                                 scale=e[:, hh, ec:ec + 1]) for hh in range(NH, 8)],
                      reads=[e], writes=[PpB[k]])
                V(P, lambda: nc.vector.scalar_tensor_tensor(
                    out=Gp[k][:].rearrange("p a b -> p (a b)"), in0=Pp[k][:].rearrange("p a b -> p (a b)"),
                    scalar=1.0, in1=Pp[k][:].rearrange("p a b -> p (a b)"), op0=ALU.is_ge, op1=ALU.mult),
                  [Pp[k], PpB[k]], [Gp[k]])

            def stage_b(itm, k):
                eg, gi, ci = itm
                s = (eg % 2) * EG + ci
                dgi = Dg[gi]
                P.mm([lambda hh=hh: nc.tensor.matmul(pgt[k][:], lhsT=Gp[k][:, hh, :], rhs=dgi[:, hh, :],
                                                     start=(hh == 0), stop=(hh == 7))
                      for hh in range(8)], reads=[Gp[k], dgi], writes=[pgt[k]])
                V(P, lambda: nc.vector.tensor_tensor(out=GA[k][:], in0=pgt[k][:], in1=AbB[s][:, gi * 128:(gi + 1) * 128], op=ALU.mult),
                  [pgt[k], AbB[s]], [GA[k]])
                P.mm([lambda nb=nb: nc.tensor.matmul(
                    py[:, nb * 512:(nb + 1) * 512], lhsT=GA[k][:], rhs=vbb[s][:, nb * 512:(nb + 1) * 512],
                    start=(ci == 0), stop=(ci == EG - 1)) for nb in range(4)],
                     reads=[GA[k], vbb[s]], writes=[py])
                if ci == EG - 1:
                    if eg == 0:
                        A(P, lambda: nc.scalar.copy(out=yacc[gi][:], in_=py[:]), [py], [yacc[gi]])
                    else:
                        V(P, lambda: nc.vector.tensor_tensor(out=yacc[gi][:], in0=py[:], in1=yacc[gi][:], op=ALU.add),
                          [py, yacc[gi]], [yacc[gi]])

            for g_i, grp in enumerate(groups):
                first = (g_i == 0)
                ng = len(grp)
                for gi, t in enumerate(grp):
                    P.dma(hAll[:, :, gi * 128:(gi + 1) * 128], self.HT.ap[t].rearrange("p (a b) -> p a b", b=128),
                          reads=[self.HT.b(t)], writes=[hAll])
                    P.dma(Eg[gi][:].rearrange("p a b -> p (a b)"), self.EE.ap[t], reads=[self.EE.b(t)], writes=[Eg[gi]])
                    P.dma(Dg[gi][:].rearrange("p a b -> p (a b)"), self.DGS.ap[t], reads=[self.DGS.b(t)], writes=[Dg[gi]])
                load_eg(0, first)
                for ci in range(EG):
                    a0(0, ci, ng)
                load_eg(1, first)
                items = [(eg, gi, ci) for eg in range(NEG_) for gi in range(ng) for ci in range(EG)]
                stage_a(items[0], 0)
                for n, itm in enumerate(items):
                    eg, gi, ci = itm
                    if n + 1 < len(items):
                        stage_a(items[n + 1], (n + 1) % 2)
                    stage_b(itm, n % 2)
                    if ci == EG - 1:
                        if eg + 1 < NEG_:
                            todo = [c2 for c2 in range(EG) if (c2 % ng) == gi]
                            for c2 in todo:
                                a0(eg + 1, c2, ng)
                        if gi == ng - 1 and eg + 2 < NEG_:
                            load_eg(eg + 2, first)
                for gi, t in enumerate(grp):
                    P.dma(self.YP.ap[t], yacc[gi][:], reads=[yacc[gi]], writes=[self.YP.b(t)])
            P.barrier()
        with ExitStack() as pes:
            P.cur = pes
            pn = self.postnorm_alloc(li, 1, 5)
            ys = [P.sb([128, D], F32) for _ in range(2)]
            for i, t in enumerate(tiles):
                y = ys[i % 2]
                P.dma(y[:], self.YP.ap[t], reads=[self.YP.b(t)], writes=[y])
                self.postnorm_tile(pn, t, y, y[:])
            P.barrier()
        P.cur = P.es

    def key_tiles(self, ctx_q):
        kt = [(0, self.TL), (1, self.TL)]
        if not ctx_q:
            kt += [(r, t) for r in range(4) for t in range(self.TL)]
        return kt

    def q_groups(self):
        gs = []
        t = 0
        while t < self.TL:
            n = min(4, self.TL - t)
            gs.append((t * 128, n * 128, False))
            t += n
        gs.append((self.TL * 128, 128, True))
        return gs

    def rope_fm(self, raw, praw, prot, Qout, nq, q0, CS, tmp1, tmp2):
        P, nc = self.P, self.nc
        A(P, lambda: nc.scalar.copy(out=raw[0:64, 0:nq], in_=praw[0:64, 0:nq]), [praw], [raw])
        P.mm([lambda: nc.tensor.matmul(prot[0:64, 0:nq], lhsT=self.Rm[0:64, 0:64], rhs=raw[0:64, 0:nq], start=True, stop=True)],
             reads=[self.Rm, raw], writes=[prot])
        V(P, lambda: nc.vector.tensor_tensor(out=tmp1[0:64, 0:nq], in0=raw[0:64, 0:nq], in1=CS[0:64, 0, q0:q0 + nq], op=ALU.mult),
          [raw, CS], [tmp1])
        V(P, lambda: nc.vector.tensor_tensor(out=tmp2[0:64, 0:nq], in0=prot[0:64, 0:nq], in1=CS[0:64, 1, q0:q0 + nq], op=ALU.mult),
          [prot, CS], [tmp2])
        V(P, lambda: nc.vector.tensor_tensor(out=Qout[0:64, 0:nq], in0=tmp1[0:64, 0:nq], in1=tmp2[0:64, 0:nq], op=ALU.add),
          [tmp1, tmp2], [Qout])

    def load_rope_consts(self):
        P, nc = self.P, self.nc
        self.Rm = P.sb([64, 64], F32)
        P.dma(self.Rm[:], self.rmat.ap[:, :], reads=[self.rmat.b()], writes=[self.Rm])
        CS = P.sb([64, 2, self.NTOK], F32)
        P.dma(CS[:], self.ropeF.ap.rearrange("a d t -> d a t"), reads=[self.ropeF.b()], writes=[CS])
        return CS

    def gather(self, locs, alls):
        P, nc = self.P, self.nc
        P.barrier()
        si = self.cc_sem
        P._wait('pool', {k: v for k, v in enumerate(P.semval) if v > 0})
        for loc, allt in zip(locs, alls):
            ins = nc.gpsimd.collective_compute("AllGather", ALU.bypass, replica_groups=[[0, 1, 2, 3], [4, 5, 6, 7]],
                                               ins=[loc.th.ap().opt()], outs=[allt.th.ap().opt()])
            P.semval[si] += 1
            ins.then_inc(P.sems[si])
        P.barrier()

    def kv1_chunks(self):
        return [(t0, min(4, self.NTI - t0)) for t0 in range(0, self.NTI, 4)]

    def mla_pre(self, li):
        P, nc = self.P, self.nc
        tiles = self.tiles_of_layer(li)
        self.ln_mod_pass(li, 0, 1, self.HT)
        with ExitStack() as pes:
            P.cur = pes
            NW = 1088
            Win = P.sb([128, KC, NW], BF16)
            stg = [P.sb([128, D], F32) for _ in range(2)]
            self.load_w_bf16(Win, self.mla_w_in.ap[0], self.mla_w_in.b(), D, NW, stg)
            QN = P.sb([128, 512], F32)
            KVN = P.sb([128, 512], F32)
            self.bcast_tile(QN, self.mla_q_norm, self.mla_q_norm.ap[0, :])
            self.bcast_tile(KVN, self.mla_kv_norm, self.mla_kv_norm.ap[0, :])
            hT = [P.sb([128, KC, 128], BF16) for _ in range(2)]
            csb = P.sb([128, NW], F32)
            st = P.sb([128, 2, 6], F32)
            mv = P.sb([128, 2, 2], F32)
            ms = P.sb([128, 2], F32)
            nb_ = [P.sb([128, 512], BF16) for _ in range(2)]
            tab = [P.sb([128, 64], F32) for _ in range(2)]
            T1 = P.sb([128, 64], F32)
            T2 = P.sb([128, 64], F32)
            krr = P.sb([128, 64], F32)
            krb = P.sb([128, 64], BF16)
            cqT = [P.sb([128, 4, 128], BF16) for _ in range(2)]
            kvl = [P.sb([128, 5, 128], BF16) for _ in range(2)]
            for k in range(2):
                G(P, lambda k=k: nc.gpsimd.memset(kvl[k][:], 0.0), [], [kvl[k]])
            pc = P.ps([128, 1536], F32)
            pT = P.ps([128, 1152], BF16)
            blocks = [(0, 512), (512, 512), (1024, 64)]
            for i, t in enumerate(tiles):
                k = i % 2
                h = hT[k]
                P.dma(h[:].rearrange("p a b -> p (a b)"), self.HT.ap[t], reads=[self.HT.b(t)], writes=[h])
                if t < self.TL:
                    P.dma(tab[k][:], self.ropeT.ap[t], reads=[self.ropeT.b()], writes=[tab[k]])
                P.mm([lambda kc=kc, c0=c0, n=n, h=h: nc.tensor.matmul(pc[:, c0:c0 + n], lhsT=h[:, kc, :], rhs=Win[:, kc, c0:c0 + n],
                                                                     start=(kc == 0), stop=(kc == KC - 1))
                      for (c0, n) in blocks for kc in range(KC)], reads=[h, Win], writes=[pc])
                A(P, lambda: nc.scalar.copy(out=csb[:], in_=pc[:, 0:NW]), [pc], [csb])
                if CUT <= 1:
                    continue
                for j in range(2):
                    V(P, lambda j=j: nc.vector.bn_stats(out=st[:, j, :], in_=csb[:, j * 512:(j + 1) * 512]), [csb], [st])
                    V(P, lambda j=j: nc.vector.bn_aggr(out=mv[:, j, :], in_=st[:, j:j + 1, :]), [st], [mv])
                V(P, lambda: nc.vector.tensor_tensor(out=ms[:], in0=mv[:, :, 0], in1=mv[:, :, 0], op=ALU.mult), [mv], [ms])
                V(P, lambda: nc.vector.tensor_tensor(out=ms[:], in0=ms[:], in1=mv[:, :, 1], op=ALU.add), [ms, mv], [ms])
                A(P, lambda: nc.scalar.activation(out=ms[:], in_=ms[:], func=AF.Sqrt, bias=self.epsb[:, 0:1], scale=1.0), [ms, self.epsb], [ms])
                V(P, lambda: nc.vector.reciprocal(out=ms[:], in_=ms[:]), [ms], [ms])
                for j, NT_ in enumerate((QN, KVN)):
                    V(P, lambda j=j, NT_=NT_: nc.vector.scalar_tensor_tensor(out=nb_[j][:], in0=csb[:, j * 512:(j + 1) * 512],
                                                                            scalar=ms[:, j:j + 1], in1=NT_[:], op0=ALU.mult, op1=ALU.mult),
                      [csb, ms, NT_], [nb_[j]])
                if CUT <= 2:
                    continue
                if t < self.TL:
                    kv4 = csb[:, 1024:1088].rearrange("p (a b d) -> p a b d", a=2, b=2)
                    tb4 = tab[k][:].rearrange("p (a b d) -> p a b d", a=2, b=2)
                    t14 = T1[:].rearrange("p (a b d) -> p a b d", a=2, b=2)
                    t24 = T2[:].rearrange("p (a b d) -> p a b d", a=2, b=2)
                    o4 = krr[:].rearrange("p (a b d) -> p a b d", a=2, b=2)
                    tk = tab[k]
                    V(P, lambda: nc.vector.tensor_tensor(out=t14[:, :, 0, :], in0=kv4[:, :, 0, :], in1=tb4[:, :, 0, :], op=ALU.mult), [csb, tk], [T1])
                    V(P, lambda: nc.vector.tensor_tensor(out=t14[:, :, 1, :], in0=kv4[:, :, 1, :], in1=tb4[:, :, 1, :], op=ALU.mult), [csb, tk], [T1])
                    V(P, lambda: nc.vector.tensor_tensor(out=t24[:, :, 0, :], in0=kv4[:, :, 0, :], in1=tb4[:, :, 1, :], op=ALU.mult), [csb, tk], [T2])
                    V(P, lambda: nc.vector.tensor_tensor(out=t24[:, :, 1, :], in0=kv4[:, :, 1, :], in1=tb4[:, :, 0, :], op=ALU.mult), [csb, tk], [T2])
                    V(P, lambda: nc.vector.tensor_tensor(out=o4[:, :, 0, :], in0=t14[:, :, 0, :], in1=t14[:, :, 1, :], op=ALU.subtract), [T1], [krr])
                    V(P, lambda: nc.vector.tensor_tensor(out=o4[:, :, 1, :], in0=t24[:, :, 0, :], in1=t24[:, :, 1, :], op=ALU.add), [T2], [krr])
                    V(P, lambda: nc.vector.tensor_copy(out=krb[:], in_=krr[:]), [krr], [krb])
                else:
                    V(P, lambda: nc.vector.tensor_copy(out=krb[:], in_=csb[:, 1024:1088]), [csb], [krb])
                if CUT <= 3:
                    continue
                P.mm([lambda j=j, c=c: nc.tensor.transpose(out=pT[:, (j * 4 + c) * 128:(j * 4 + c + 1) * 128],
                                                           in_=nb_[j][:, c * 128:(c + 1) * 128], identity=self.ident[:])
                      for j in range(2) for c in range(4)] +
                     [lambda: nc.tensor.transpose(out=pT[0:64, 1024:1152], in_=krb[:, 0:64], identity=self.ident[:])],
                     reads=[nb_[0], nb_[1], krb, self.ident], writes=[pT])
                if CUT <= 4:
                    continue
                A(P, lambda k=k: nc.scalar.copy(out=cqT[k][:].rearrange("p a b -> p (a b)"), in_=pT[:, 0:512]), [pT], [cqT[k]])
                A(P, lambda k=k: nc.scalar.copy(out=kvl[k][:, 0:4, :].rearrange("p a b -> p (a b)"), in_=pT[:, 512:1024]), [pT], [kvl[k]])
                A(P, lambda k=k: nc.scalar.copy(out=kvl[k][0:64, 4, :], in_=pT[0:64, 1024:1152]), [pT], [kvl[k]])
                if CUT <= 5:
                    continue
                P.dma(self.CQT.ap[t * 128:(t + 1) * 128, :], cqT[k][:].rearrange("p a b -> p (a b)"), reads=[cqT[k]], writes=[self.CQT.b(('w', t))])
                if CUT <= 6:
                    continue
                P.dma(self.KV1_loc[t // 4].ap[(t % 4) * 128:(t % 4 + 1) * 128, :], kvl[k][:].rearrange("p a b -> p (a b)"), reads=[kvl[k]],
                      writes=[self.KV1_loc[t // 4].b(('w', t))])
            P.barrier()
        P.cur = P.es

    def mla_attn(self, li):
        P, nc = self.P, self.nc
        TL, NTI, NTOK = self.TL, self.NTI, self.NTOK
        with ExitStack() as pes:
            P.cur = pes
            CS = self.load_rope_consts()
            ktl = self.key_tiles(False)
            NKT = len(ktl)
            NK = NKT * 128
            KVall = P.sb([128, 5, NK], BF16)
            for kt, (r, t) in enumerate(ktl):
                ch = self.kv1_chunks()[t // 4]
                row = (r * ch[1] + (t % 4)) * 128
                P.dma(KVall[:, :, kt * 128:(kt + 1) * 128], self.KV1_all[t // 4].ap[row:row + 128, :].rearrange("p (a b) -> p a b", b=128),
                      reads=[self.KV1_all[t // 4].b()], writes=[KVall])
            cq = P.sb([128, 4, NTOK], BF16)
            for t in range(NTI):
                P.dma(cq[:, :, t * 128:(t + 1) * 128], self.CQT.ap[t * 128:(t + 1) * 128, :].rearrange("p (a b) -> p a b", b=128),
                      reads=[self.CQT.b()], writes=[cq])
            wkf = P.sb([128, 4, 256], F32)
            wqf = P.sb([128, 4, 192], F32)
            Wk = P.sb([128, 4, 256], BF16)
            Wq = P.sb([128, 4, 192], BF16)
            Kh = P.sb([128, NK], BF16)
            Vh = P.sb([128, NKT, 128], BF16)
            Qn = P.sb([128, 512], BF16)
            Qr = P.sb([64, 512], BF16)
            raw = P.sb([64, 512], F32)
            tmp1 = P.sb([64, 512], F32)
            tmp2 = P.sb([64, 512], F32)
            PT = [P.sb([128, 512], BF16) for _ in range(3)]
            rL = P.sb([128, 512], F32)
            Lacc = P.sb([128, 512], F32)
            ao = [P.sb([128, 512], BF16) for _ in range(2)]
            pS = [P.ps([128, 512], F32) for _ in range(3)]
            pO = P.ps([128, 512], F32)
            pL = P.ps([128, 512], F32)
            pX = [P.ps([128, 512], F32) for _ in range(2)]
            wukv = self.mla_w_ukv.ap[0].rearrange("(c p) n -> p c n", p=128)
            wuq = self.mla_w_uq.ap[0].rearrange("(c p) n -> p c n", p=128)
            nao = 0
            for h in range(16):
                P.dma(wkf[:], wukv[:, :, h * 256:(h + 1) * 256], reads=[self.mla_w_ukv.b()], writes=[wkf])
                P.dma(wqf[:], wuq[:, :, h * 192:(h + 1) * 192], reads=[self.mla_w_uq.b()], writes=[wqf])
                G(P, lambda: nc.gpsimd.tensor_copy(out=Wk[:], in_=wkf[:]), [wkf], [Wk])
                G(P, lambda: nc.gpsimd.tensor_copy(out=Wq[:], in_=wqf[:]), [wqf], [Wq])
                xi = 0
                for k0 in range(0, NK, 512):
                    n = min(512, NK - k0)
                    px = pX[xi % 2]
                    xi += 1
                    P.mm([lambda c=c, px=px, k0=k0, n=n: nc.tensor.matmul(px[:, 0:n], lhsT=Wk[:, c, 0:128], rhs=KVall[:, c, k0:k0 + n],
                                                                         start=(c == 0), stop=(c == 3)) for c in range(4)],
                         reads=[Wk, KVall], writes=[px])
                    V(P, lambda px=px, k0=k0, n=n: nc.vector.tensor_copy(out=Kh[:, k0:k0 + n], in_=px[:, 0:n]), [px], [Kh])
                for kt0 in range(0, NKT, 4):
                    n = min(4, NKT - kt0)
                    px = pX[xi % 2]
                    xi += 1
                    P.mm([lambda c=c, j=j, px=px, kt0=kt0: nc.tensor.matmul(px[:, j * 128:(j + 1) * 128],
                                                                           lhsT=KVall[:, c, (kt0 + j) * 128:(kt0 + j + 1) * 128],
                                                                           rhs=Wk[:, c, 128:256], start=(c == 0), stop=(c == 3))
                          for j in range(n) for c in range(4)], reads=[Wk, KVall], writes=[px])
                    V(P, lambda px=px, kt0=kt0, n=n: nc.vector.tensor_copy(out=Vh[:, kt0:kt0 + n, :].rearrange("p a b -> p (a b)"),
                                                                          in_=px[:, 0:n * 128]), [px], [Vh])
                for (q0, nq, isctx) in self.q_groups():
                    px = pX[xi % 2]
                    xi += 1
                    P.mm([lambda c=c, px=px: nc.tensor.matmul(px[:, 0:nq], lhsT=Wq[:, c, 0:128], rhs=cq[:, c, q0:q0 + nq],
                                                              start=(c == 0), stop=(c == 3)) for c in range(4)],
                         reads=[Wq, cq], writes=[px])
                    V(P, lambda px=px: nc.vector.tensor_copy(out=Qn[:, 0:nq], in_=px[:, 0:nq]), [px], [Qn])
                    px2 = pX[xi % 2]
                    xi += 1
                    P.mm([lambda c=c, px2=px2: nc.tensor.matmul(px2[0:64, 0:nq], lhsT=Wq[:, c, 128:192], rhs=cq[:, c, q0:q0 + nq],
                                                                start=(c == 0), stop=(c == 3)) for c in range(4)],
                         reads=[Wq, cq], writes=[px2])
                    if isctx:
                        V(P, lambda px2=px2: nc.vector.tensor_copy(out=Qr[0:64, 0:nq], in_=px2[0:64, 0:nq]), [px2], [Qr])
                    else:
                        px3 = pX[xi % 2]
                        xi += 1
                        self.rope_fm(raw, px2, px3, Qr, nq, q0, CS, tmp1, tmp2)
                    kts = list(range(2)) if isctx else list(range(NKT))

                    halves = [(0, nq)] if nq <= 256 else [(0, nq // 2), (nq // 2, nq)]

                    def s_stage(kt, k2):
                        P.mm([f for (a_, b_) in halves for f in (
                            lambda a_=a_, b_=b_: nc.tensor.matmul(pS[k2][:, a_:b_], lhsT=Kh[:, kt * 128:(kt + 1) * 128], rhs=Qn[:, a_:b_],
                                                                  start=True, stop=False),
                            lambda a_=a_, b_=b_: nc.tensor.matmul(pS[k2][:, a_:b_], lhsT=KVall[0:64, 4, kt * 128:(kt + 1) * 128],
                                                                  rhs=Qr[0:64, a_:b_], start=False, stop=True))],
                             reads=[Kh, Qn, KVall, Qr], writes=[pS[k2]])
                    s_stage(kts[0], 0)
                    if len(kts) > 1:
                        s_stage(kts[1], 1)
                    for ii, kt in enumerate(kts):
                        k2 = ii % 3
                        if ii + 2 < len(kts):
                            s_stage(kts[ii + 2], (ii + 2) % 3)
                        A(P, lambda k2=k2: nc.scalar.activation(out=PT[k2][:, 0:nq], in_=pS[k2][:, 0:nq], func=AF.Exp, scale=MLA_SCALE),
                          [pS[k2]], [PT[k2]])
                        first, last = (ii == 0), (ii == len(kts) - 1)
                        P.mm([lambda kt=kt, k2=k2, a_=a_, b_=b_: nc.tensor.matmul(pO[:, a_:b_], lhsT=Vh[:, kt, :], rhs=PT[k2][:, a_:b_],
                                                                                  start=first, stop=last) for (a_, b_) in halves],
                             reads=[Vh, PT[k2]], writes=[pO])
                        if first:
                            V(P, lambda k2=k2: nc.vector.tensor_copy(out=Lacc[:, 0:nq], in_=PT[k2][:, 0:nq]), [PT[k2]], [Lacc])
                        else:
                            V(P, lambda k2=k2: nc.vector.tensor_tensor(out=Lacc[:, 0:nq], in0=Lacc[:, 0:nq], in1=PT[k2][:, 0:nq], op=ALU.add),
                              [Lacc, PT[k2]], [Lacc])
                    P.mm([lambda: nc.tensor.matmul(pL[:, 0:nq], lhsT=self.onesf[:], rhs=Lacc[:, 0:nq], start=True, stop=True)],
                         reads=[self.onesf, Lacc], writes=[pL])
                    V(P, lambda: nc.vector.reciprocal(out=rL[:, 0:nq], in_=pL[:, 0:nq]), [pL], [rL])
                    a = ao[nao % 2]
                    nao += 1
                    V(P, lambda a=a: nc.vector.tensor_tensor(out=a[:, 0:nq], in0=pO[:, 0:nq], in1=rL[:, 0:nq], op=ALU.mult), [pO, rL], [a])
                    P.dma(self.AOT.ap[h, :, q0:q0 + nq], a[:, 0:nq], reads=[a], writes=[self.AOT.b(('w', h, q0))])
            P.barrier()
        P.cur = P.es

    def da_pre(self, li):
        P, nc = self.P, self.nc
        TL, NTI, NTOK = self.TL, self.NTI, self.NTOK
        tiles = self.tiles_of_layer(li)
        self.ln_mod_pass(li, 0, 1, self.HT)
        with ExitStack() as pes:
            P.cur = pes
            Wv = P.sb([128, KC, D], BF16)
            stg = [P.sb([128, D], F32) for _ in range(2)]
            self.load_w_bf16(Wv, self.da_w_in.ap[0], self.da_w_in.b(), D, D, stg, c0=2 * D)
            hT = [P.sb([128, KC, 128], BF16) for _ in range(2)]
            vb = [P.sb([128, D], BF16) for _ in range(2)]
            pv = P.ps([128, D], F32)
            for i, t in enumerate(tiles):
                k = i % 2
                h = hT[k]
                P.dma(h[:].rearrange("p a b -> p (a b)"), self.HT.ap[t], reads=[self.HT.b(t)], writes=[h])
                P.mm([lambda kc=kc, nb=nb, h=h: nc.tensor.matmul(pv[:, nb * 512:(nb + 1) * 512], lhsT=h[:, kc, :],
                                                                 rhs=Wv[:, kc, nb * 512:(nb + 1) * 512],
                                                                 start=(kc == 0), stop=(kc == KC - 1))
                      for nb in range(4) for kc in range(KC)], reads=[h, Wv], writes=[pv])
                A(P, lambda k=k: nc.scalar.copy(out=vb[k][:], in_=pv[:]), [pv], [vb[k]])
                for hh in range(16):
                    P.dma(self.V_loc[hh].ap[:, t * 128:(t + 1) * 128], vb[k][:, hh * 128:(hh + 1) * 128], reads=[vb[k]],
                          writes=[self.V_loc[hh].b(('w', t))])
            P.barrier()
        for which in ("k", "q"):
            with ExitStack() as pes:
                P.cur = pes
                CS = self.load_rope_consts()
                Ww = P.sb([128, KC, D], BF16)
                stg = [P.sb([128, D], F32) for _ in range(2)]
                self.load_w_bf16(Ww, self.da_w_in.ap[0], self.da_w_in.b(), D, D, stg, c0=(D if which == "k" else 0))
                hA = P.sb([128, KC, NTOK], BF16)
                for t in tiles:
                    P.dma(hA[:, :, t * 128:(t + 1) * 128], self.HT.ap[t].rearrange("p (a b) -> p a b", b=128), reads=[self.HT.b(t)], writes=[hA])
                raw = P.sb([64, 512], F32)
                tmp1 = P.sb([64, 512], F32)
                tmp2 = P.sb([64, 512], F32)
                ob = [P.sb([64, 512], BF16) for _ in range(2)]
                pr = [P.ps([128, 512], F32) for _ in range(2)]
                prot = [P.ps([128, 512], F32) for _ in range(2)]
                n_o = 0
                for hc in range(32):
                    for (q0, nq, isctx) in self.q_groups():
                        k = n_o % 2
                        n_o += 1
                        P.mm([lambda kc=kc, k=k: nc.tensor.matmul(pr[k][0:64, 0:nq], lhsT=Ww[:, kc, hc * 64:(hc + 1) * 64],
                                                                   rhs=hA[:, kc, q0:q0 + nq], start=(kc == 0), stop=(kc == KC - 1))
                              for kc in range(KC)], reads=[Ww, hA], writes=[pr[k]])
                        if isctx:
                            V(P, lambda k=k: nc.vector.tensor_copy(out=ob[k][0:64, 0:nq], in_=pr[k][0:64, 0:nq]), [pr[k]], [ob[k]])
                        else:
                            self.rope_fm(raw, pr[k], prot[k], ob[k], nq, q0, CS, tmp1, tmp2)
                        if which == "k":
                            dk = self.KT_loc[hc // 2]
                            P.dma(dk.ap[(hc % 2) * 64:(hc % 2 + 1) * 64, q0:q0 + nq], ob[k][0:64, 0:nq], reads=[ob[k]], writes=[dk.b(('w', hc, q0))])
                        else:
                            P.dma(self.QT.ap[hc * 64:(hc + 1) * 64, q0:q0 + nq], ob[k][0:64, 0:nq], reads=[ob[k]],
                                  writes=[self.QT.b(('w', hc, q0))])
                P.barrier()
        P.cur = P.es

    def da_attn(self, li):
        P, nc = self.P, self.nc
        TL, NTI, NTOK = self.TL, self.NTI, self.NTOK
        lam_init = 0.8 - 0.6 * math.exp(-0.3 * li)
        with ExitStack() as pes:
            P.cur = pes
            lp = P.sb([1, 4, 64], F32)
            pr2 = P.sb([1, 2, 64], F32)
            s2 = P.sb([1, 2], F32)
            lam1 = P.sb([1, 1], F32)
            NEGLAM = P.sb([128, 1], F32)
            SUBL = P.sb([128, 1], F32)
            pS = [[P.ps([128, 512], F32) for _ in range(3)] for _ in range(2)]
            pl = pS[0][0]
            P.dma(lp[:], self.da_lambda.ap[0:1, :, :], reads=[self.da_lambda.b()], writes=[lp])
            lp4 = lp[:].rearrange("p (a b) d -> p a b d", b=2)
            V(P, lambda: nc.vector.tensor_tensor(out=pr2[:], in0=lp4[:, :, 0, :], in1=lp4[:, :, 1, :], op=ALU.mult), [lp], [pr2])
            V(P, lambda: nc.vector.tensor_reduce(out=s2[:], in_=pr2[:], axis=mybir.AxisListType.X, op=ALU.add), [pr2], [s2])
            A(P, lambda: nc.scalar.activation(out=s2[:], in_=s2[:], func=AF.Exp), [s2], [s2])
            V(P, lambda: nc.vector.scalar_tensor_tensor(out=lam1[:], in0=s2[:, 1:2], scalar=-lam_init, in1=s2[:, 0:1],
                                                        op0=ALU.add, op1=ALU.subtract), [s2], [lam1])
            P.mm([lambda: nc.tensor.matmul(pl[:, 0:1], lhsT=self.onesf[0:1, :], rhs=lam1[:], start=True, stop=True)],
                 reads=[self.onesf, lam1], writes=[pl])
            V(P, lambda: nc.vector.tensor_copy(out=NEGLAM[:], in_=pl[:, 0:1]), [pl], [NEGLAM])
            P.dma(SUBL[:], self.da_subln.ap[0, :].rearrange("(p o) -> p o", o=1), reads=[self.da_subln.b()], writes=[SUBL])
            V(P, lambda: nc.vector.tensor_scalar(out=SUBL[:], in0=SUBL[:], scalar1=(1.0 - lam_init), scalar2=None, op0=ALU.mult), [SUBL], [SUBL])
            ktl = self.key_tiles(False)
            NKT = len(ktl)
            NK = NKT * 128
            K12 = [P.sb([64, NK], BF16) for _ in range(2)]
            Vh = P.sb([128, NKT, 128], BF16)
            Q12 = [P.sb([64, NTOK], BF16) for _ in range(2)]
            PT = [[P.sb([128, 512], BF16) for _ in range(3)] for _ in range(2)]
            r1 = P.sb([128, 512], F32)
            Lacc = [P.sb([128, 512], F32) for _ in range(2)]
            oa = P.sb([128, 512], F32)
            ob = P.sb([128, 512], F32)
            sq = P.sb([128, 512], F32)
            ao = [P.sb([128, 512], BF16) for _ in range(2)]
            pO = [P.ps([128, 512], F32) for _ in range(2)]
            pL = [pS[0][1], pS[1][1]]
            nao = 0
            for h in range(16):
                for cmp_ in range(2):
                    hc = 2 * h + cmp_
                    Kc = K12[cmp_]
                    kall = self.KT_all[h]
                    for r in range(2):
                        P.dma(Kc[:, r * 128:(r + 1) * 128], kall.ap[r * 128 + cmp_ * 64:r * 128 + (cmp_ + 1) * 64, TL * 128:NTI * 128],
                              reads=[kall.b()], writes=[Kc])
                    for r in range(4):
                        P.dma(Kc[:, (2 + r * TL) * 128:(2 + (r + 1) * TL) * 128], kall.ap[r * 128 + cmp_ * 64:r * 128 + (cmp_ + 1) * 64, 0:TL * 128],
                              reads=[kall.b()], writes=[Kc])
                    P.dma(Q12[cmp_][:], self.QT.ap[hc * 64:(hc + 1) * 64, :], reads=[self.QT.b()], writes=[Q12[cmp_]])
                vall = self.V_all[h]
                for r in range(2):
                    P.dma(Vh[:, r, :], vall.ap[r * 128:(r + 1) * 128, TL * 128:NTI * 128], reads=[vall.b()], writes=[Vh])
                for r in range(4):
                    P.dma(Vh[:, 2 + r * TL:2 + (r + 1) * TL, :].rearrange("p a b -> p (a b)"),
                          vall.ap[r * 128:(r + 1) * 128, 0:TL * 128], reads=[vall.b()], writes=[Vh])
                for (q0, nq, isctx) in self.q_groups():
                    kts = list(range(2)) if isctx else list(range(NKT))

                    halves = [(0, nq)] if nq <= 256 else [(0, nq // 2), (nq // 2, nq)]

                    def s_stage(kt, k2):
                        for cmp_ in range(2):
                            P.mm([lambda cmp_=cmp_, a_=a_, b_=b_: nc.tensor.matmul(pS[cmp_][k2][:, a_:b_], lhsT=K12[cmp_][:, kt * 128:(kt + 1) * 128],
                                                                                    rhs=Q12[cmp_][:, q0 + a_:q0 + b_], start=True, stop=True)
                                  for (a_, b_) in halves],
                                 reads=[K12[cmp_], Q12[cmp_]], writes=[pS[cmp_][k2]])
                    s_stage(kts[0], 0)
                    if len(kts) > 1:
                        s_stage(kts[1], 1)
                    for ii, kt in enumerate(kts):
                        k2 = ii % 3
                        if ii + 2 < len(kts):
                            s_stage(kts[ii + 2], (ii + 2) % 3)
                        first, last = (ii == 0), (ii == len(kts) - 1)
                        for cmp_ in range(2):
                            A(P, lambda cmp_=cmp_, k2=k2: nc.scalar.activation(out=PT[cmp_][k2][:, 0:nq], in_=pS[cmp_][k2][:, 0:nq],
                                                                               func=AF.Exp, scale=DA_SCALE), [pS[cmp_][k2]], [PT[cmp_][k2]])
                            P.mm([lambda cmp_=cmp_, kt=kt, k2=k2, a_=a_, b_=b_: nc.tensor.matmul(pO[cmp_][:, a_:b_], lhsT=Vh[:, kt, :],
                                                                                                 rhs=PT[cmp_][k2][:, a_:b_], start=first, stop=last)
                                  for (a_, b_) in halves],
                                 reads=[Vh, PT[cmp_][k2]], writes=[pO[cmp_]])
                            if first:
                                V(P, lambda cmp_=cmp_, k2=k2: nc.vector.tensor_copy(out=Lacc[cmp_][:, 0:nq], in_=PT[cmp_][k2][:, 0:nq]),
                                  [PT[cmp_][k2]], [Lacc[cmp_]])
                            else:
                                V(P, lambda cmp_=cmp_, k2=k2: nc.vector.tensor_tensor(out=Lacc[cmp_][:, 0:nq], in0=Lacc[cmp_][:, 0:nq],
                                                                                      in1=PT[cmp_][k2][:, 0:nq], op=ALU.add),
                                  [Lacc[cmp_], PT[cmp_][k2]], [Lacc[cmp_]])
                    for cmp_ in range(2):
                        P.mm([lambda cmp_=cmp_: nc.tensor.matmul(pL[cmp_][:, 0:nq], lhsT=self.onesf[:], rhs=Lacc[cmp_][:, 0:nq], start=True, stop=True)],
                             reads=[self.onesf, Lacc[cmp_]], writes=[pL[cmp_]])
                    V(P, lambda: nc.vector.reciprocal(out=r1[:, 0:nq], in_=pL[0][:, 0:nq]), [pL[0]], [r1])
                    V(P, lambda: nc.vector.tensor_tensor(out=oa[:, 0:nq], in0=pO[0][:, 0:nq], in1=r1[:, 0:nq], op=ALU.mult), [pO[0], r1], [oa])
                    V(P, lambda: nc.vector.reciprocal(out=r1[:, 0:nq], in_=pL[1][:, 0:nq]), [pL[1]], [r1])
                    V(P, lambda: nc.vector.tensor_tensor(out=ob[:, 0:nq], in0=pO[1][:, 0:nq], in1=r1[:, 0:nq], op=ALU.mult), [pO[1], r1], [ob])
                    V(P, lambda: nc.vector.scalar_tensor_tensor(out=oa[:, 0:nq], in0=ob[:, 0:nq], scalar=NEGLAM[:, 0:1], in1=oa[:, 0:nq],
                                                                op0=ALU.mult, op1=ALU.add), [ob, NEGLAM, oa], [oa])
                    V(P, lambda: nc.vector.tensor_tensor(out=sq[:, 0:nq], in0=oa[:, 0:nq], in1=oa[:, 0:nq], op=ALU.mult), [oa], [sq])
                    pss = pS[0][0]
                    P.mm([lambda: nc.tensor.matmul(pss[:, 0:nq], lhsT=self.onesf[:], rhs=sq[:, 0:nq], start=True, stop=True)],
                         reads=[self.onesf, sq], writes=[pss])
                    A(P, lambda: nc.scalar.activation(out=r1[:, 0:nq], in_=pss[:, 0:nq], func=AF.Sqrt, bias=self.epsb[:, 0:1], scale=1.0 / 128.0),
                      [pss, self.epsb], [r1])
                    V(P, lambda: nc.vector.reciprocal(out=r1[:, 0:nq], in_=r1[:, 0:nq]), [r1], [r1])
                    V(P, lambda: nc.vector.tensor_tensor(out=oa[:, 0:nq], in0=oa[:, 0:nq], in1=r1[:, 0:nq], op=ALU.mult), [oa, r1], [oa])
                    a = ao[nao % 2]
                    nao += 1
                    V(P, lambda a=a: nc.vector.tensor_scalar(out=a[:, 0:nq], in0=oa[:, 0:nq], scalar1=SUBL[:, 0:1], scalar2=None, op0=ALU.mult),
                      [oa, SUBL], [a])
                    P.dma(self.AOT.ap[h, :, q0:q0 + nq], a[:, 0:nq], reads=[a], writes=[self.AOT.b(('w', h, q0))])
            P.barrier()
        P.cur = P.es

    def attn_out(self, li, w_dbuf):
        self.outproj_pass(li, w_dbuf.ap[0], w_dbuf.b(), self.AOT,
                          lambda t: self.AOT.ap[:, :, t * 128:(t + 1) * 128].rearrange("h p t -> p h t"))

    ALL_STEPS = [
        [('mod', 0), ('gmlp', 0), ('peer', 0), ('mod', 1), ('mla_pre', 1)],
        [('mod', 1), ('mla_attn', 1), ('mla_out', 1), ('peer', 1), ('mod', 2), ('da_pre', 2)],
        [('mod', 2), ('da_attn', 2), ('da_out', 2), ('peer', 2), ('mod', 3), ('gmlp', 3), ('peer', 3)],
    ]

    def plan(self):
        ph = self.phase
        if ph == 'all':
            steps = self.ALL_STEPS[0] + [('gather_mla', 1)] + self.ALL_STEPS[1] + [('gather_da', 2)] + self.ALL_STEPS[2]
        else:
            steps = list(self.ALL_STEPS[ph])
        steps = [st for st in steps if st[1] in self.layers and st[0] not in self.skip]
        self.steps = steps
        self.ML = sorted({l for (k, l) in steps if k == 'mod'})
        self.PL = sorted({l for (k, l) in steps if k == 'peer'})
        self.GL = sorted({l for (k, l) in steps if k == 'gmlp'})
        self.midx = {l: i for i, l in enumerate(self.ML)}
        self.lidx = {l: i for i, l in enumerate(self.PL)}
        self.gidx = {l: i for i, l in enumerate(self.GL)}
        self.kinds = {k for (k, l) in steps}

    def build(self):
        nc = self.nc
        TL, NTI, NTOK = self.TL, self.NTI, self.NTOK
        self.plan()
        kinds = self.kinds
        self.xin = self.din("xin", [NTI, 128, D])
        self.cT = self.din("cT", [128, KC, 2])
        self.mod_w = self.din("mod_w", [max(1, len(self.ML)), D, 6 * D])
        self.mod_b = self.din("mod_b", [DEPTH, 6 * D])
        self.ln_g = self.din("ln_g", [DEPTH, 2, D])
        self.ln_b = self.din("ln_b", [DEPTH, 2, D])
        if self.PL:
            npl = len(self.PL)
            self.peer_wq = self.din("peer_wq", [npl, D, D])
            self.peer_k1T = self.din("peer_k1T", [npl, 128, 128])
            self.peer_k2T = self.din("peer_k2T", [npl, 128, 128])
            self.peer_uL = self.din("peer_uL", [npl, 128, 128, KC, 128])
            self.peer_v = self.din("peer_v", [npl, 16384, D])
            self.EE = self.dscr("EE", [NTI, 128, D])
            self.DGS = self.dscr("DGS", [NTI, 128, 1024], BF16)
            self.YP = self.dscr("YP", [NTI, 128, D])
            self.UB = self.dscr("UB", [128, 128, D], BF16)
            self.VB = self.dscr("VB", [128, 128, D], BF16)
        if self.GL:
            ng = len(self.GL)
            self.gm_w_in = self.din("gm_w_in", [ng, D, 2 * D])
            self.gm_ln_g = self.din("gm_ln_g", [ng, D])
            self.gm_ln_b = self.din("gm_ln_b", [ng, D])
            self.gm_wsT = self.din("gm_wsT", [ng, 16, 128, 128])
            self.gm_bs = self.din("gm_bs", [ng, 16, 128])
            self.gm_w_out = self.din("gm_w_out", [ng, D, D])
            self.VN = self.dscr("VN", [NTI, 128, D], BF16)
            self.MT = self.dscr("MT", [NTI, 128, D], BF16)
        if kinds & {'mla_pre', 'mla_attn', 'da_pre', 'da_attn'}:
            self.ropeT = self.din("ropeT", [NTI, 128, 64])
            self.ropeF = self.din("ropeF", [2, 64, NTOK])
            self.rmat = self.din("rmat", [64, 64])
            self.AOT = self.dscr("AOT", [16, 128, NTOK], BF16)
        if 'mla_pre' in kinds:
            self.mla_w_in = self.din("mla_w_in", [1, D, 1088])
            self.mla_q_norm = self.din("mla_q_norm", [1, 512])
            self.mla_kv_norm = self.din("mla_kv_norm", [1, 512])
        if 'mla_attn' in kinds:
            self.mla_w_uq = self.din("mla_w_uq", [1, 512, 3072])
            self.mla_w_ukv = self.din("mla_w_ukv", [1, 512, 4096])
        if 'mla_out' in kinds:
            self.mla_w_out = self.din("mla_w_out", [1, D, D])
        if kinds & {'da_pre'}:
            self.da_w_in = self.din("da_w_in", [1, D, 3 * D])
        if 'da_attn' in kinds:
            self.da_lambda = self.din("da_lambda", [1, 4, 64])
            self.da_subln = self.din("da_subln", [1, 128])
        if 'da_out' in kinds:
            self.da_w_out = self.din("da_w_out", [1, D, D])
        fused = (self.phase == 'all')
        if kinds & {'mla_pre', 'mla_attn'}:
            self.CQT = self.xphase("CQT", [NTOK, 512], BF16, 0, 1)
            mk_loc = self.dscr if fused else self.dout
            mk_all = self.dscr if fused else self.din
            if fused or 'mla_pre' in kinds:
                self.KV1_loc = [mk_loc("KV1_loc_%d" % j, [n * 128, 640], BF16) for j, (t0, n) in enumerate(self.kv1_chunks())]
            if fused or 'mla_attn' in kinds:
                self.KV1_all = [mk_all("KV1_all_%d" % j, [4 * n * 128, 640], BF16) for j, (t0, n) in enumerate(self.kv1_chunks())]
        if kinds & {'da_pre', 'da_attn'}:
            self.QT = self.xphase("QT", [D, NTOK], BF16, 1, 2)
            mk_loc = self.dscr if fused else self.dout
            mk_all = self.dscr if fused else self.din
            if fused or 'da_pre' in kinds:
                self.KT_loc = [mk_loc("KT_loc_%d" % hh, [128, NTOK], BF16) for hh in range(16)]
                self.V_loc = [mk_loc("V_loc_%d" % hh, [128, NTOK], BF16) for hh in range(16)]
            if fused or 'da_attn' in kinds:
                self.KT_all = [mk_all("KT_all_%d" % hh, [4 * 128, NTOK], BF16) for hh in range(16)]
                self.V_all = [mk_all("V_all_%d" % hh, [4 * 128, NTOK], BF16) for hh in range(16)]
        self.xout = self.dout("xout", [NTI, 128, D])
        self.X = self.dscr("X", [NTI, 128, D])
        self.MOD = self.dscr("MOD", [DEPTH, 2, 6 * D])
        self.HT = self.dscr("HT", [NTI, 128, D], BF16)
        with ExitStack() as es:
            self.P = P = Prog(nc, es)
            self.cc_sem = P.new_sem("cc")
            self.consts()
            with ExitStack() as pes:
                P.cur = pes
                xt = [P.sb([128, D], F32) for _ in range(2)]
                for t in range(NTI):
                    P.dma(xt[t % 2][:], self.xin.ap[t], reads=[self.xin.b()], writes=[xt[t % 2]])
                    P.dma(self.X.ap[t], xt[t % 2][:], reads=[xt[t % 2]], writes=[self.X.b(t)])
                P.barrier()
            P.cur = P.es
            for (kind, li) in self.steps:
                if kind == 'mod':
                    self.mod_pass(li)
                elif kind == 'gmlp':
                    self.gmlp(li, self.gidx[li])
                elif kind == 'peer':
                    self.peer(li)
                elif kind == 'mla_pre':
                    self.mla_pre(li)
                elif kind == 'gather_mla':
                    self.gather(self.KV1_loc, self.KV1_all)
                elif kind == 'mla_attn':
                    self.mla_attn(li)
                elif kind == 'mla_out':
                    self.attn_out(li, self.mla_w_out)
                elif kind == 'da_pre':
                    self.da_pre(li)
                elif kind == 'gather_da':
                    self.gather(self.KT_loc + self.V_loc, self.KT_all + self.V_all)
                elif kind == 'da_attn':
                    self.da_attn(li)
                elif kind == 'da_out':
                    self.attn_out(li, self.da_w_out)
            xt = [P.sb([128, D], F32) for _ in range(2)]
            for t in range(NTI):
                P.dma(xt[t % 2][:], self.X.ap[t], reads=[self.X.b(t)], writes=[xt[t % 2]])
                P.dma(self.xout.ap[t], xt[t % 2][:], reads=[xt[t % 2]], writes=[self.xout.b(t)])
            P.barrier()
        return nc


def _npdt(dt):
    return ml_dtypes.bfloat16 if dt == BF16 else np.float32


def prep_shared(inputs, bld):
    f = lambda a: np.ascontiguousarray(np.asarray(a, dtype=np.float32))
    need = bld.inputs
    sh = {}
    ML, PL, GL = bld.ML, bld.PL, [l // 3 for l in bld.GL]
    if "mod_w" in need:
        sh["mod_w"] = f(np.asarray(inputs["mod_w"])[ML or [0]])
    sh["mod_b"] = f(inputs["mod_b"])
    sh["ln_g"] = f(inputs["ln_g"])
    sh["ln_b"] = f(inputs["ln_b"])
    if PL:
        sh["peer_wq"] = f(np.asarray(inputs["peer_wq"])[PL])
        sh["peer_k1T"] = f(np.asarray(inputs["peer_k1"])[PL].transpose(0, 2, 1))
        sh["peer_k2T"] = f(np.asarray(inputs["peer_k2"])[PL].transpose(0, 2, 1))
        u = np.asarray(inputs["peer_u"])[PL]
        sh["peer_uL"] = f(u.reshape(len(PL), 128, 128, KC, 128).transpose(0, 1, 4, 3, 2))
        sh["peer_v"] = f(np.asarray(inputs["peer_v"])[PL])
    if GL:
        sh["gm_w_in"] = f(np.asarray(inputs["gm_w_in"])[GL])
        sh["gm_ln_g"] = f(np.asarray(inputs["gm_ln_g"])[GL])
        sh["gm_ln_b"] = f(np.asarray(inputs["gm_ln_b"])[GL])
        sh["gm_wsT"] = f(np.asarray(inputs["gm_ws"])[GL].transpose(0, 1, 3, 2))
        sh["gm_bs"] = f(np.asarray(inputs["gm_bs"])[GL])
        sh["gm_w_out"] = f(np.asarray(inputs["gm_w_out"])[GL])
    for nm in ["mla_w_in", "mla_q_norm", "mla_kv_norm", "mla_w_uq", "mla_w_ukv", "mla_w_out",
               "da_w_in", "da_lambda", "da_subln", "da_w_out"]:
        if nm in need:
            sh[nm] = f(inputs[nm])
    if "rmat" in need:
        R = np.zeros((64, 64), np.float32)
        for base in (0, 32):
            for a in range(16):
                R[base + a, base + a + 16] = -1.0
                R[base + a + 16, base + a] = 1.0
        sh["rmat"] = np.ascontiguousarray(R.T)
    return sh


def rope_tables(c, TL):
    q = c % 4
    n = TL * 128
    pos = (q * n + np.arange(n)).astype(np.int64)
    row = (pos // GRID_W).astype(np.float32)
    col = (pos % GRID_W).astype(np.float32)
    nf = 16
    inv = (np.float32(10000.0) ** (-np.arange(nf, dtype=np.float32) / np.float32(nf))).astype(np.float32)
    ang_r = (row[:, None] * inv).astype(np.float32)
    ang_c = (col[:, None] * inv).astype(np.float32)
    cr, sr, cc, sc = np.cos(ang_r), np.sin(ang_r), np.cos(ang_c), np.sin(ang_c)
    NTOK = (TL + 1) * 128
    ropeT = np.zeros((TL + 1, 128, 64), np.float32)
    ropeT[:TL] = np.concatenate([cr, sr, cc, sc], -1).reshape(TL, 128, 64)
    ropeF = np.zeros((2, 64, NTOK), np.float32)
    ropeF[0, :, :n] = np.concatenate([cr, cr, cc, cc], -1).T
    ropeF[1, :, :n] = np.concatenate([sr, sr, sc, sc], -1).T
    ropeF[0, :, n:] = 1.0
    return {"ropeT": ropeT, "ropeF": ropeF}


def prep_core(inputs, c, TL):
    b, q = c // 4, c % 4
    x = np.asarray(inputs["x"], dtype=np.float32)
    ctx = np.asarray(inputs["ctx"], dtype=np.float32)
    xin = np.zeros((TL + 1, 128, D), np.float32)
    xin[:TL] = x[b, q * TL * 128:(q + 1) * TL * 128].reshape(TL, 128, D)
    if q < 2:
        xin[TL] = ctx[b, q * 128:(q + 1) * 128]
    cc = np.stack([np.asarray(inputs["c"], np.float32)[b], np.asarray(inputs["c_ctx"], np.float32)], -1)
    cT = np.ascontiguousarray(cc.reshape(KC, 128, 2).transpose(1, 0, 2))
    return {"xin": xin, "cT": cT}


FUSED = True
TL_FULL = 16


def run_model(inputs, TL, fused, layers=(0, 1, 2, 3), ncores=8, skip=()):
    phases = ['all'] if fused else [0, 1, 2]
    state = [prep_core(inputs, c, TL) for c in range(ncores)]
    ropes = [rope_tables(c, TL) for c in range(ncores)]
    for ph in phases:
        bld = Builder(TL, ph, layers, skip)
        bld.plan()
        if not bld.steps:
            continue
        bld = Builder(TL, ph, layers, skip)
        nc = bld.build()
        sh = prep_shared(inputs, bld)
        in_maps = []
        for c in range(ncores):
            m = dict(sh)
            for nm in bld.inputs:
                if nm in state[c]:
                    m[nm] = state[c][nm]
                elif nm in ropes[c]:
                    m[nm] = ropes[c][nm]
            in_maps.append({nm: m[nm] for nm in bld.inputs})
        print("[run_model] phase", ph, "steps", bld.steps, flush=True)
        res = run_bass_kernel_spmd(nc, in_maps, core_ids=list(range(ncores)))
        outs = res.results
        for c in range(ncores):
            state[c]["xin"] = np.asarray(outs[c]["xout"])
            for nm in ("CQT", "QT"):
                if nm in outs[c]:
                    state[c][nm] = np.asarray(outs[c][nm])
        for nm_loc in list(outs[0].keys()):
            if "_loc_" not in nm_loc:
                continue
            nm_all = nm_loc.replace("_loc_", "_all_")
            for g0 in range(0, ncores, 4):
                cat = np.concatenate([np.asarray(outs[c][nm_loc]) for c in range(g0, min(g0 + 4, ncores))], axis=0)
                for c in range(g0, min(g0 + 4, ncores)):
                    state[c][nm_all] = cat
    return state


def kernel(**inputs):
    TL = TL_FULL
    state = run_model(inputs, TL, FUSED)
    x = np.asarray(inputs["x"])
    out = np.zeros(x.shape, np.float32)
    for c in range(8):
        b, q = c // 4, c % 4
        out[b, q * TL * 128:(q + 1) * TL * 128] = state[c]["xin"][:TL].reshape(TL * 128, D)
    return out
```

```python
import math
import os
CUT = int(os.environ.get('MK_CUT', '99'))
from contextlib import ExitStack

import numpy as np
import ml_dtypes
import concourse.bass as bass
import concourse.mybir as mybir
from concourse.bass_utils import run_bass_kernel_spmd

F32 = mybir.dt.float32
BF16 = mybir.dt.bfloat16
AF = mybir.ActivationFunctionType
ALU = mybir.AluOpType

D = 2048
KC = 16
DEPTH = 4
ALPHA = (2.0 * DEPTH) ** 0.25
LN_EPS = 1e-6
GRID_W = 64
MLA_SCALE = (128 + 64) ** -0.5
DA_SCALE = 64 ** -0.5
NEG = -1.0e30


class Buf:
    __slots__ = ("t", "w", "r")

    def __init__(self, t=None):
        self.t = t
        self.w = None
        self.r = {}

    def __getitem__(self, idx):
        return self.t[idx]


class DBuf:
    def __init__(self, ap):
        self.ap = ap
        self.bufs = {}

    def b(self, key=0):
        if key not in self.bufs:
            self.bufs[key] = Buf()
        return self.bufs[key]

    def all(self):
        return list(self.bufs.values())


class Prog:
    NDMA = 24

    def __init__(self, nc, es):
        self.nc = nc
        self.es = es
        self.cur = es
        self.eng = {'pe': nc.tensor, 'dve': nc.vector, 'act': nc.scalar, 'pool': nc.gpsimd, 'sp': nc.sync}
        self.sems = []
        self.semval = []
        self.engsem = {e: self.new_sem("s_" + e) for e in self.eng}
        self.waited = {e: {} for e in self.eng}
        self.dma_sems = [self.new_sem("d%d" % i) for i in range(self.NDMA)]
        self.dma_rr = 0
        self.nb = 0

    def new_sem(self, name):
        s = self.es.enter_context(self.nc.semaphore(name))
        self.sems.append(s)
        self.semval.append(0)
        return len(self.sems) - 1

    def sb(self, shape, dt):
        self.nb += 1
        t = self.cur.enter_context(self.nc.sbuf_tensor("sb%d" % self.nb, list(shape), dt))
        return Buf(t)

    def ps(self, shape, dt):
        self.nb += 1
        t = self.cur.enter_context(self.nc.psum_tensor("ps%d" % self.nb, list(shape), dt))
        return Buf(t)

    def _deps(self, reads, writes):
        need = {}
        for b in reads:
            if b.w is not None and need.get(b.w[0], 0) < b.w[1]:
                need[b.w[0]] = b.w[1]
        for b in writes:
            if b.w is not None and need.get(b.w[0], 0) < b.w[1]:
                need[b.w[0]] = b.w[1]
            for si, v in b.r.items():
                if need.get(si, 0) < v:
                    need[si] = v
        return need

    def _wait(self, e, need, skip_self=False):
        own = self.engsem.get(e)
        w = self.waited[e]
        for si, v in need.items():
            if skip_self and si == own:
                continue
            if w.get(si, 0) < v:
                self.eng[e].wait_ge(self.sems[si], v)
                w[si] = v

    def _mark(self, ev, reads, writes):
        si, v = ev
        for b in reads:
            if b.r.get(si, 0) < v:
                b.r[si] = v
        for b in writes:
            b.w = ev
            b.r = {}

    def op(self, e, fn, reads=(), writes=()):
        self._wait(e, self._deps(reads, writes), skip_self=(e == 'pe'))
        ins = fn()
        si = self.engsem[e]
        self.semval[si] += 1
        ins.then_inc(self.sems[si], 1)
        ev = (si, self.semval[si])
        self._mark(ev, reads, writes)
        return ev

    def mm(self, fns, reads=(), writes=()):
        self._wait('pe', self._deps(reads, writes), skip_self=True)
        ins = None
        for fn in fns:
            ins = fn()
        si = self.engsem['pe']
        self.semval[si] += 1
        ins.then_inc(self.sems[si], 1)
        ev = (si, self.semval[si])
        self._mark(ev, reads, writes)
        return ev

    def dma(self, out, in_, reads=(), writes=(), q='sp', **kw):
        si = self.dma_sems[self.dma_rr % self.NDMA]
        self.dma_rr += 1
        need = self._deps(reads, writes)
        if self.semval[si] > 0 and need.get(si, 0) < self.semval[si]:
            need[si] = self.semval[si]
        self._wait(q, need)
        ins = self.eng[q].dma_start(out=out, in_=in_, **kw)
        self.semval[si] += 16
        ins.then_inc(self.sems[si], 16)
        ev = (si, self.semval[si])
        self._mark(ev, reads, writes)
        return ev

    def barrier(self):
        need = {si: v for si, v in enumerate(self.semval) if v > 0}
        for e in self.eng:
            self._wait(e, need)


def V(P, fn, r, w):
    return P.op('dve', fn, r, w)


def A(P, fn, r, w):
    return P.op('act', fn, r, w)


def G(P, fn, r, w):
    return P.op('pool', fn, r, w)


class Builder:
    def __init__(self, TL, phase, layers=(0, 1, 2, 3), skip=()):
        self.skip = set(skip)
        self.TL = TL
        self.NTI = TL + 1
        self.NTOK = self.NTI * 128
        self.phase = phase
        self.layers = layers
        self.nc = bass.Bass("TRN2", target_bir_lowering=False)
        self.inputs = {}
        self.outputs = {}
        self.mod_done = set()

    def din(self, name, shape, dt=F32):
        t = self.nc.dram_tensor(name, list(shape), dt, kind="ExternalInput")
        self.inputs[name] = (tuple(shape), dt)
        return DBuf(t.ap())

    def dout(self, name, shape, dt=F32):
        t = self.nc.dram_tensor(name, list(shape), dt, kind="ExternalOutput")
        self.outputs[name] = (tuple(shape), dt)
        return DBuf(t.ap())

    def dscr(self, name, shape, dt=F32):
        t = self.nc.dram_tensor(name, list(shape), dt)
        d = DBuf(t.ap())
        d.th = t
        return d

    def xphase(self, name, shape, dt, prod, cons):
        ph = self.phase
        if ph == 'all' or prod == cons:
            return self.dscr(name, shape, dt)
        if ph == prod:
            return self.dout(name, shape, dt)
        if ph == cons:
            return self.din(name, shape, dt)
        return None

    def consts(self):
        P, nc = self.P, self.nc
        self.identf = P.sb([128, 128], F32)
        self.ident = P.sb([128, 128], BF16)
        self.onesb = P.sb([128, 128], BF16)
        self.onesf = P.sb([128, 128], F32)
        self.epsb = P.sb([128, 1], F32)
        G(P, lambda: nc.gpsimd.memset(self.identf[:], 1.0), [], [self.identf])
        G(P, lambda: nc.gpsimd.affine_select(out=self.identf[:], in_=self.identf[:], pattern=[[-1, 128]],
                                             compare_op=ALU.is_equal, fill=0.0, base=0, channel_multiplier=1),
          [self.identf], [self.identf])
        A(P, lambda: nc.scalar.copy(out=self.ident[:], in_=self.identf[:]), [self.identf], [self.ident])
        G(P, lambda: nc.gpsimd.memset(self.onesb[:], 1.0), [], [self.onesb])
        G(P, lambda: nc.gpsimd.memset(self.onesf[:], 1.0), [], [self.onesf])
        G(P, lambda: nc.gpsimd.memset(self.epsb[:], LN_EPS), [], [self.epsb])

    def bcast_tile(self, dst, src_dbuf, row_ap, key=0):
        self.P.dma(dst[:], row_ap.partition_broadcast(128), reads=[src_dbuf.b(key)], writes=[dst])

    def load_w_bf16(self, wb, w_ap, wbuf, K, N, stages, c0=0):
        P, nc = self.P, self.nc
        for kc in range(K // 128):
            st = stages[kc % len(stages)]
            P.dma(st[:, 0:N], w_ap[kc * 128:(kc + 1) * 128, c0:c0 + N], reads=[wbuf], writes=[st])
            G(P, lambda st=st, kc=kc: nc.gpsimd.tensor_copy(out=wb[:, kc, :], in_=st[:, 0:N]), [st], [wb])

    def ln_stats(self, x, st, mv, rs, n=2048):
        P, nc = self.P, self.nc
        nchunk = n // 512
        for c in range(nchunk):
            V(P, lambda c=c: nc.vector.bn_stats(out=st[:, c, :], in_=x[:, c * 512:(c + 1) * 512]), [x], [st])
        V(P, lambda: nc.vector.bn_aggr(out=mv[:], in_=st[:, 0:nchunk, :]), [st], [mv])
        A(P, lambda: nc.scalar.activation(out=rs[:], in_=mv[:, 1:2], func=AF.Sqrt, bias=self.epsb[:, 0:1], scale=1.0),
          [mv, self.epsb], [rs])
        V(P, lambda: nc.vector.reciprocal(out=rs[:], in_=rs[:]), [rs], [rs])

    def tiles_of_layer(self, li):
        return list(range(self.TL)) if li == DEPTH - 1 else list(range(self.NTI))

    def mod_pass(self, li):
        if li in self.mod_done:
            return
        self.mod_done.add(li)
        P, nc = self.P, self.nc
        with ExitStack() as pes:
            P.cur = pes
            cT = P.sb([128, KC, 2], F32)
            sT = P.sb([128, KC, 2], F32)
            wst = [P.sb([128, KC, 512], F32) for _ in range(2)]
            bia = [P.sb([2, 512], F32) for _ in range(2)]
            res = [P.sb([2, 512], F32) for _ in range(2)]
            pm = [P.ps([2, 512], F32) for _ in range(2)]
            P.dma(cT[:], self.cT.ap[:, :, :], reads=[self.cT.b()], writes=[cT])
            A(P, lambda: nc.scalar.activation(out=sT[:], in_=cT[:], func=AF.Silu), [cT], [sT])
            wv = self.mod_w.ap[self.midx[li]].rearrange("(kc p) n -> p kc n", p=128)
            for nb in range(24):
                w = wst[nb % 2]
                bi = bia[nb % 2]
                rr = res[nb % 2]
                pp = pm[nb % 2]
                P.dma(w[:], wv[:, :, nb * 512:(nb + 1) * 512], reads=[self.mod_w.b()], writes=[w])
                P.dma(bi[:], self.mod_b.ap[li, nb * 512:(nb + 1) * 512].partition_broadcast(2),
                      reads=[self.mod_b.b()], writes=[bi])
                P.mm([lambda kc=kc, w=w, pp=pp: nc.tensor.matmul(pp[:], lhsT=sT[:, kc, :], rhs=w[:, kc, :],
                                                                   start=(kc == 0), stop=(kc == KC - 1))
                      for kc in range(KC)], reads=[sT, w], writes=[pp])
                seg = nb // 4
                add1 = 1.0 if seg in (1, 4) else 0.0
                V(P, lambda pp=pp, bi=bi, rr=rr, add1=add1: nc.vector.scalar_tensor_tensor(
                    out=rr[:], in0=pp[:], scalar=add1, in1=bi[:], op0=ALU.add, op1=ALU.add), [pp, bi], [rr])
                P.dma(self.MOD.ap[li, :, nb * 512:(nb + 1) * 512], rr[:], reads=[rr], writes=[self.MOD.b(li)])
            P.barrier()
        P.cur = P.es

    def mod_row(self, li, m, seg):
        return self.MOD.ap[li, m, seg * 2048:(seg + 1) * 2048]

    def ln_mod_pass(self, li, seg_shift, seg_scale, HT):
        P, nc = self.P, self.nc
        tiles = self.tiles_of_layer(li)
        with ExitStack() as pes:
            P.cur = pes
            SC = [P.sb([128, D], F32) for _ in range(2)]
            SH = [P.sb([128, D], F32) for _ in range(2)]
            for m in range(2):
                self.bcast_tile(SC[m], self.MOD, self.mod_row(li, m, seg_scale), li)
                self.bcast_tile(SH[m], self.MOD, self.mod_row(li, m, seg_shift), li)
            xs = [P.sb([128, D], F32) for _ in range(2)]
            wk = [P.sb([128, D], F32) for _ in range(2)]
            hb = [P.sb([128, D], BF16) for _ in range(2)]
            hT = [P.sb([128, KC, 128], BF16) for _ in range(2)]
            st = [P.sb([128, 4, 6], F32) for _ in range(2)]
            mv = [P.sb([128, 2], F32) for _ in range(2)]
            rs = [P.sb([128, 1], F32) for _ in range(2)]
            pT = [P.ps([128, D], BF16) for _ in range(2)]
            for i, t in enumerate(tiles):
                k = i % 2
                m = 1 if t == self.TL else 0
                x = xs[k]
                P.dma(x[:], self.X.ap[t], reads=[self.X.b(t)], writes=[x])
                self.ln_stats(x, st[k], mv[k], rs[k])
                V(P, lambda x=x, k=k: nc.vector.tensor_scalar(out=wk[k][:], in0=x[:], scalar1=mv[k][:, 0:1],
                                                              scalar2=rs[k][:, 0:1], op0=ALU.subtract, op1=ALU.mult),
                  [x, mv[k], rs[k]], [wk[k]])
                G(P, lambda k=k, m=m: nc.gpsimd.tensor_tensor(out=wk[k][:], in0=wk[k][:], in1=SC[m][:], op=ALU.mult),
                  [wk[k], SC[m]], [wk[k]])
                G(P, lambda k=k, m=m: nc.gpsimd.tensor_tensor(out=hb[k][:], in0=wk[k][:], in1=SH[m][:], op=ALU.add),
                  [wk[k], SH[m]], [hb[k]])
                P.mm([lambda c=c, k=k: nc.tensor.transpose(out=pT[k][:, c * 128:(c + 1) * 128],
                                                           in_=hb[k][:, c * 128:(c + 1) * 128], identity=self.ident[:])
                      for c in range(KC)], reads=[hb[k], self.ident], writes=[pT[k]])
                A(P, lambda k=k: nc.scalar.copy(out=hT[k][:].rearrange("p a b -> p (a b)"), in_=pT[k][:]), [pT[k]], [hT[k]])
                P.dma(HT.ap[t], hT[k][:].rearrange("p a b -> p (a b)"), reads=[hT[k]], writes=[HT.b(t)])
            P.barrier()
        P.cur = P.es

    def postnorm_alloc(self, li, sub, seg_gate):
        P = self.P
        pn = {}
        pn['G'] = [P.sb([128, D], F32) for _ in range(2)]
        pn['LG'] = P.sb([128, D], F32)
        pn['LB'] = P.sb([128, D], F32)
        for m in range(2):
            self.bcast_tile(pn['G'][m], self.MOD, self.mod_row(li, m, seg_gate), li)
        self.bcast_tile(pn['LG'], self.ln_g, self.ln_g.ap[li, sub, :])
        self.bcast_tile(pn['LB'], self.ln_b, self.ln_b.ap[li, sub, :])
        pn['x'] = [P.sb([128, D], F32) for _ in range(2)]
        pn['z'] = [P.sb([128, D], F32) for _ in range(2)]
        pn['st'] = [P.sb([128, 4, 6], F32) for _ in range(2)]
        pn['mv'] = [P.sb([128, 2], F32) for _ in range(2)]
        pn['rs'] = [P.sb([128, 1], F32) for _ in range(2)]
        pn['i'] = 0
        return pn

    def postnorm_tile(self, pn, t, ybuf, yap):
        P, nc = self.P, self.nc
        k = pn['i'] % 2
        pn['i'] += 1
        m = 1 if t == self.TL else 0
        x, z, st, mv, rs = pn['x'][k], pn['z'][k], pn['st'][k], pn['mv'][k], pn['rs'][k]
        P.dma(x[:], self.X.ap[t], reads=[self.X.b(t)], writes=[x])
        V(P, lambda: nc.vector.tensor_tensor(out=z[:], in0=yap, in1=pn['G'][m][:], op=ALU.mult), [ybuf, pn['G'][m]], [z])
        V(P, lambda: nc.vector.scalar_tensor_tensor(out=z[:], in0=x[:], scalar=ALPHA, in1=z[:], op0=ALU.mult, op1=ALU.add),
          [x, z], [z])
        self.ln_stats(z, st, mv, rs)
        V(P, lambda: nc.vector.tensor_scalar(out=z[:], in0=z[:], scalar1=mv[:, 0:1], scalar2=rs[:, 0:1],
                                             op0=ALU.subtract, op1=ALU.mult), [z, mv, rs], [z])
        G(P, lambda: nc.gpsimd.tensor_tensor(out=z[:], in0=z[:], in1=pn['LG'][:], op=ALU.mult), [z, pn['LG']], [z])
        G(P, lambda: nc.gpsimd.tensor_tensor(out=x[:], in0=z[:], in1=pn['LB'][:], op=ALU.add), [z, pn['LB']], [x])
        P.dma(self.X.ap[t], x[:], reads=[x], writes=[self.X.b(t)])

    def outproj_pass(self, li, w_ap, wbuf, AT, at_view):
        P, nc = self.P, self.nc
        tiles = self.tiles_of_layer(li)
        with ExitStack() as pes:
            P.cur = pes
            Wo = P.sb([128, KC, D], BF16)
            stg = [P.sb([128, D], F32) for _ in range(2)]
            self.load_w_bf16(Wo, w_ap, wbuf, D, D, stg)
            pn = self.postnorm_alloc(li, 0, 2)
            aT = [P.sb([128, KC, 128], BF16) for _ in range(2)]
            py = P.ps([128, D], F32)
            for i, t in enumerate(tiles):
                a = aT[i % 2]
                P.dma(a[:], at_view(t), reads=[AT.b(t)], writes=[a])
                P.mm([lambda g=g, nb=nb, a=a: nc.tensor.matmul(py[:, nb * 512:(nb + 1) * 512], lhsT=a[:, g, :],
                                                              rhs=Wo[:, g, nb * 512:(nb + 1) * 512],
                                                              start=(g == 0), stop=(g == KC - 1))
                      for nb in range(4) for g in range(KC)], reads=[a, Wo], writes=[py])
                self.postnorm_tile(pn, t, py, py[:])
            P.barrier()
        P.cur = P.es

    def gmlp(self, li, j):
        P, nc = self.P, self.nc
        tiles = self.tiles_of_layer(li)
        self.ln_mod_pass(li, 0, 1, self.HT)
        with ExitStack() as pes:
            P.cur = pes
            Wv = P.sb([128, KC, D], BF16)
            stg = [P.sb([128, D], F32) for _ in range(2)]
            self.load_w_bf16(Wv, self.gm_w_in.ap[j], self.gm_w_in.b(), D, D, stg, c0=D)
            GG = P.sb([128, D], F32)
            GB = P.sb([128, D], F32)
            self.bcast_tile(GG, self.gm_ln_g, self.gm_ln_g.ap[j, :])
            self.bcast_tile(GB, self.gm_ln_b, self.gm_ln_b.ap[j, :])
            hT = [P.sb([128, KC, 128], BF16) for _ in range(2)]
            v = [P.sb([128, D], F32) for _ in range(2)]
            vb = [P.sb([128, D], BF16) for _ in range(2)]
            st = [P.sb([128, 4, 6], F32) for _ in range(2)]
            mv = [P.sb([128, 2], F32) for _ in range(2)]
            rs = [P.sb([128, 1], F32) for _ in range(2)]
            pv = P.ps([128, D], F32)
            for i, t in enumerate(tiles):
                k = i % 2
                h = hT[k]
                P.dma(h[:].rearrange("p a b -> p (a b)"), self.HT.ap[t], reads=[self.HT.b(t)], writes=[h])
                P.mm([lambda kc=kc, nb=nb, h=h: nc.tensor.matmul(pv[:, nb * 512:(nb + 1) * 512], lhsT=h[:, kc, :],
                                                                 rhs=Wv[:, kc, nb * 512:(nb + 1) * 512],
                                                                 start=(kc == 0), stop=(kc == KC - 1))
                      for nb in range(4) for kc in range(KC)], reads=[h, Wv], writes=[pv])
                A(P, lambda k=k: nc.scalar.activation(out=v[k][:], in_=pv[:], func=AF.Gelu_apprx_tanh), [pv], [v[k]])
                self.ln_stats(v[k], st[k], mv[k], rs[k])
                V(P, lambda k=k: nc.vector.tensor_scalar(out=v[k][:], in0=v[k][:], scalar1=mv[k][:, 0:1],
                                                         scalar2=rs[k][:, 0:1], op0=ALU.subtract, op1=ALU.mult),
                  [v[k], mv[k], rs[k]], [v[k]])
                G(P, lambda k=k: nc.gpsimd.tensor_tensor(out=v[k][:], in0=v[k][:], in1=GG[:], op=ALU.mult), [v[k], GG], [v[k]])
                G(P, lambda k=k: nc.gpsimd.tensor_tensor(out=vb[k][:], in0=v[k][:], in1=GB[:], op=ALU.add), [v[k], GB], [vb[k]])
                P.dma(self.VN.ap[t], vb[k][:], reads=[vb[k]], writes=[self.VN.b(t)])
            P.barrier()
        with ExitStack() as pes:
            P.cur = pes
            Wu = P.sb([128, KC, D], BF16)
            stg = [P.sb([128, D], F32) for _ in range(2)]
            self.load_w_bf16(Wu, self.gm_w_in.ap[j], self.gm_w_in.b(), D, D, stg, c0=0)
            wsf = P.sb([128, KC, 128], F32)
            wsT = P.sb([128, KC, 128], BF16)
            P.dma(wsf[:], self.gm_wsT.ap[j].rearrange("g q p -> q g p"), reads=[self.gm_wsT.b()], writes=[wsf])
            G(P, lambda: nc.gpsimd.tensor_copy(out=wsT[:], in_=wsf[:]), [wsf], [wsT])
            bsT = P.sb([128, D], F32)
            self.bcast_tile(bsT, self.gm_bs, self.gm_bs.ap[j].rearrange("g p -> (g p)"))
            hT = [P.sb([128, KC, 128], BF16) for _ in range(2)]
            vb = [P.sb([128, D], BF16) for _ in range(2)]
            uT = [P.sb([128, D], F32) for _ in range(2)]
            t1 = [P.sb([128, D], F32) for _ in range(2)]
            mT = [P.sb([128, D], BF16) for _ in range(2)]
            pu = P.ps([128, D], F32)
            psv = P.ps([128, D], F32)
            for i, t in enumerate(tiles):
                k = i % 2
                h = hT[k]
                P.dma(h[:].rearrange("p a b -> p (a b)"), self.HT.ap[t], reads=[self.HT.b(t)], writes=[h])
                P.dma(vb[k][:], self.VN.ap[t], reads=[self.VN.b(t)], writes=[vb[k]])
                P.mm([lambda kc=kc, g=g, h=h: nc.tensor.matmul(pu[:, g * 128:(g + 1) * 128],
                                                               lhsT=Wu[:, kc, g * 128:(g + 1) * 128], rhs=h[:, kc, :],
                                                               start=(kc == 0), stop=(kc == KC - 1))
                      for g in range(KC) for kc in range(KC)], reads=[h, Wu], writes=[pu])
                A(P, lambda k=k: nc.scalar.activation(out=uT[k][:], in_=pu[:], func=AF.Gelu_apprx_tanh), [pu], [uT[k]])
                P.mm([lambda g=g, k=k: nc.tensor.matmul(psv[:, g * 128:(g + 1) * 128],
                                                        lhsT=vb[k][:, g * 128:(g + 1) * 128], rhs=wsT[:, g, :],
                                                        start=True, stop=True)
                      for g in range(KC)], reads=[vb[k], wsT], writes=[psv])
                V(P, lambda k=k: nc.vector.tensor_tensor(out=t1[k][:], in0=psv[:], in1=bsT[:], op=ALU.add), [psv, bsT], [t1[k]])
                V(P, lambda k=k: nc.vector.tensor_tensor(out=mT[k][:], in0=t1[k][:], in1=uT[k][:], op=ALU.mult),
                  [t1[k], uT[k]], [mT[k]])
                P.dma(self.MT.ap[t], mT[k][:], reads=[mT[k]], writes=[self.MT.b(t)])
            P.barrier()
        self.outproj_pass(li, self.gm_w_out.ap[j], self.gm_w_out.b(), self.MT, lambda t: self.MT.ap[t].rearrange("p (a b) -> p a b", b=128))

    def peer(self, li):
        P, nc = self.P, self.nc
        tiles = self.tiles_of_layer(li)
        self.ln_mod_pass(li, 3, 4, self.HT)
        with ExitStack() as pes:
            P.cur = pes
            Wq = P.sb([128, KC, D], BF16)
            stg = [P.sb([128, D], F32) for _ in range(2)]
            self.load_w_bf16(Wq, self.peer_wq.ap[self.lidx[li]], self.peer_wq.b(), D, D, stg)
            kTf = P.sb([128, 2, 128], F32)
            kT = P.sb([128, 2, 128], BF16)
            P.dma(kTf[:, 0, :], self.peer_k1T.ap[self.lidx[li]], reads=[self.peer_k1T.b()], writes=[kTf])
            P.dma(kTf[:, 1, :], self.peer_k2T.ap[self.lidx[li]], reads=[self.peer_k2T.b()], writes=[kTf])
            G(P, lambda: nc.gpsimd.tensor_copy(out=kT[:], in_=kTf[:]), [kTf], [kT])
            hT = [P.sb([128, KC, 128], BF16) for _ in range(2)]
            qT = P.sb([128, KC, 128], BF16)
            S = P.sb([128, 16, 128], F32)
            TMP = P.sb([128, 128], F32)
            V16 = P.sb([128, 16, 16], F32)
            C = P.sb([128, 8, 256], F32)
            T2 = P.sb([128, 256], F32)
            T3 = P.sb([128, 256], F32)
            B24 = P.sb([128, 8, 24], F32)
            Dm = P.sb([128, 8, 16], F32)
            Z = P.sb([128, 8], F32)
            TAU = P.sb([128, 8], F32)
            TH = P.sb([128, 8], F32)
            W8 = P.sb([128, 8], F32)
            E = [P.sb([128, 8, 256], F32) for _ in range(2)]
            DG = [P.sb([128, 8, 128], BF16) for _ in range(2)]
            pq = P.ps([128, D], F32)
            pS = P.ps([128, D], F32)
            S2 = [S, P.sb([128, 16, 128], F32)]

            def pe_a(i2):
                h = hT[i2 % 2]
                P.dma(h[:].rearrange("p a b -> p (a b)"), self.HT.ap[tiles[i2]], reads=[self.HT.b(tiles[i2])], writes=[h])
                P.mm([lambda kc=kc, n=n: nc.tensor.matmul(pq[:, n * 128:(n + 1) * 128],
                                                          lhsT=Wq[:, kc, n * 128:(n + 1) * 128], rhs=h[:, kc, :],
                                                          start=(kc == 0), stop=(kc == KC - 1))
                      for n in range(16) for kc in range(KC)], reads=[h, Wq], writes=[pq])

            def tail(i2):
                Sx = S2[i2 % 2]
                A(P, lambda: nc.scalar.copy(out=qT[:].rearrange("p a b -> p (a b)"), in_=pq[:]), [pq], [qT])
                P.mm([lambda n=n: nc.tensor.matmul(pS[:, n * 128:(n + 1) * 128], lhsT=qT[:, n, :], rhs=kT[:, n % 2, :],
                                                   start=True, stop=True) for n in range(16)],
                     reads=[qT, kT], writes=[pS])
                A(P, lambda: nc.scalar.copy(out=Sx[:].rearrange("p a b -> p (a b)"), in_=pS[:]), [pS], [Sx])

            pe_a(0)
            tail(0)
            for i, t in enumerate(tiles):
                k = i % 2
                S = S2[k]
                if i + 1 < len(tiles):
                    pe_a(i + 1)
                for n in range(16):
                    V(P, lambda n=n: nc.vector.max(out=V16[:, n, 0:8], in_=S[:, n, :]), [S], [V16])
                    V(P, lambda n=n: nc.vector.match_replace(out=TMP[:], in_to_replace=V16[:, n, 0:8], in_values=S[:, n, :],
                                                             imm_value=NEG), [S, V16], [TMP])
                    V(P, lambda n=n: nc.vector.max(out=V16[:, n, 8:16], in_=TMP[:]), [TMP], [V16])
                Vv = V16[:].rearrange("p (h two) a -> p h two a", two=2)
                in0 = Vv[:, :, 0, :].unsqueeze(3).to_broadcast([128, 8, 16, 16])
                in1 = Vv[:, :, 1, :].unsqueeze(2).to_broadcast([128, 8, 16, 16])
                V(P, lambda: nc.vector.tensor_tensor(out=C[:].rearrange("p h (a b) -> p h a b", b=16), in0=in0, in1=in1,
                                                     op=ALU.add), [V16], [C])
                for hh in range(8):
                    V(P, lambda hh=hh: nc.vector.max(out=B24[:, hh, 0:8], in_=C[:, hh, :]), [C], [B24])
                    V(P, lambda hh=hh: nc.vector.match_replace(out=T2[:], in_to_replace=B24[:, hh, 0:8], in_values=C[:, hh, :],
                                                               imm_value=NEG), [C, B24], [T2])
                    V(P, lambda hh=hh: nc.vector.max(out=B24[:, hh, 8:16], in_=T2[:]), [T2], [B24])
                    V(P, lambda hh=hh: nc.vector.match_replace(out=T3[:], in_to_replace=B24[:, hh, 8:16], in_values=T2[:],
                                                               imm_value=NEG), [T2, B24], [T3])
                    V(P, lambda hh=hh: nc.vector.max(out=B24[:, hh, 16:24], in_=T3[:]), [T3], [B24])
                V(P, lambda: nc.vector.tensor_tensor(out=Dm[:], in0=B24[:, :, 0:16],
                                                     in1=B24[:, :, 0:1].to_broadcast([128, 8, 16]), op=ALU.subtract),
                  [B24], [Dm])
                A(P, lambda: nc.scalar.activation(out=Dm[:], in_=Dm[:], func=AF.Exp), [Dm], [Dm])
                V(P, lambda: nc.vector.tensor_reduce(out=Z[:], in_=Dm[:], axis=mybir.AxisListType.X, op=ALU.add), [Dm], [Z])
                V(P, lambda: nc.vector.tensor_tensor(out=TAU[:], in0=B24[:, :, 15], in1=B24[:, :, 16], op=ALU.add), [B24], [TAU])
                V(P, lambda: nc.vector.tensor_scalar(out=TH[:], in0=TAU[:], scalar1=-0.25, scalar2=None, op0=ALU.mult), [TAU], [TH])
                V(P, lambda: nc.vector.scalar_tensor_tensor(out=W8[:], in0=TAU[:], scalar=0.5, in1=B24[:, :, 0],
                                                            op0=ALU.mult, op1=ALU.subtract), [TAU, B24], [W8])
                A(P, lambda: nc.scalar.activation(out=W8[:], in_=W8[:], func=AF.Exp), [W8], [W8])
                V(P, lambda: nc.vector.reciprocal(out=Z[:], in_=Z[:]), [Z], [Z])
                V(P, lambda: nc.vector.tensor_tensor(out=W8[:], in0=W8[:], in1=Z[:], op=ALU.mult), [W8, Z], [W8])
                e = E[k]
                V(P, lambda e=e: nc.vector.tensor_tensor(out=e[:], in0=S[:].rearrange("p (h two) i -> p h (two i)", two=2),
                                                         in1=TH[:].unsqueeze(2).to_broadcast([128, 8, 256]), op=ALU.add),
                  [S, TH], [e])
                A(P, lambda e=e: nc.scalar.activation(out=e[:], in_=e[:], func=AF.Exp), [e], [e])
                dg = DG[k]
                for hh in range(8):
                    V(P, lambda hh=hh, dg=dg: nc.vector.tensor_scalar(out=dg[:, hh, :], in0=self.identf[:],
                                                                      scalar1=W8[:, hh:hh + 1], scalar2=None, op0=ALU.mult),
                      [self.identf, W8], [dg])
                P.dma(self.EE.ap[t], e[:].rearrange("p a b -> p (a b)"), reads=[e], writes=[self.EE.b(t)])
                P.dma(self.DGS.ap[t], dg[:].rearrange("p a b -> p (a b)"), reads=[dg], writes=[self.DGS.b(t)])
                if i + 1 < len(tiles):
                    tail(i + 1)
            P.barrier()
        GS = 4
        EG = 4
        NEC = 128
        NEG_ = NEC // EG
        groups = [tiles[i:i + GS] for i in range(0, len(tiles), GS)]
        lw = self.lidx[li]
        with ExitStack() as pes:
            P.cur = pes
            yacc = [P.sb([128, D], F32) for _ in range(GS)]
            hAll = P.sb([128, KC, GS * 128], BF16)
            Eg = [P.sb([128, 8, 256], F32) for _ in range(GS)]
            Dg = [P.sb([128, 8, 128], BF16) for _ in range(GS)]
            ust = P.sb([128, KC, 128], F32)
            vst = P.sb([128, D], F32)
            ub = [P.sb([128, KC, 128], BF16) for _ in range(2 * EG)]
            vbb = [P.sb([128, D], BF16) for _ in range(2 * EG)]
            AbB = [P.sb([128, GS * 128], BF16) for _ in range(2 * EG)]
            Pp = [P.sb([128, 8, 128], F32) for _ in range(2)]
            Gp = [P.sb([128, 8, 128], BF16) for _ in range(2)]
            GA = [P.sb([128, 128], BF16) for _ in range(2)]
            pactB = [P.ps([128, 512], F32) for _ in range(2)]
            pgt = [P.ps([128, 128], F32) for _ in range(2)]
            py = P.ps([128, D], F32)
            a0n = [0]

            def load_eg(eg, first):
                for ci in range(EG):
                    ec = eg * EG + ci
                    s = (eg % 2) * EG + ci
                    if first:
                        P.dma(ust[:], self.peer_uL.ap[lw, ec], reads=[self.peer_uL.b()], writes=[ust])
                        A(P, lambda s=s: nc.scalar.copy(out=ub[s][:], in_=ust[:]), [ust], [ub[s]])
                        P.dma(self.UB.ap[ec], ub[s][:].rearrange("p a b -> p (a b)"), reads=[ub[s]], writes=[self.UB.b(ec)])
                        P.dma(vst[:], self.peer_v.ap[lw, ec * 128:(ec + 1) * 128, :], reads=[self.peer_v.b()], writes=[vst])
                        A(P, lambda s=s: nc.scalar.copy(out=vbb[s][:], in_=vst[:]), [vst], [vbb[s]])
                        P.dma(self.VB.ap[ec], vbb[s][:], reads=[vbb[s]], writes=[self.VB.b(ec)])
                    else:
                        P.dma(ub[s][:].rearrange("p a b -> p (a b)"), self.UB.ap[ec], reads=[self.UB.b(ec)], writes=[ub[s]])
                        P.dma(vbb[s][:], self.VB.ap[ec], reads=[self.VB.b(ec)], writes=[vbb[s]])

            def a0(eg, ci, ng):
                s = (eg % 2) * EG + ci
                k = a0n[0] % 2
                a0n[0] += 1
                n = ng * 128
                P.mm([lambda kc=kc: nc.tensor.matmul(pactB[k][:, 0:n], lhsT=ub[s][:, kc, :], rhs=hAll[:, kc, 0:n],
                                                     start=(kc == 0), stop=(kc == KC - 1))
                      for kc in range(KC)], reads=[ub[s], hAll], writes=[pactB[k]])
                A(P, lambda: nc.scalar.activation(out=AbB[s][:, 0:n], in_=pactB[k][:, 0:n], func=AF.Gelu_apprx_tanh),
                  [pactB[k]], [AbB[s]])

            def stage_a(itm, k):
                eg, gi, ci = itm
                ec = eg * EG + ci
                e = Eg[gi]
                V(P, lambda: nc.vector.tensor_tensor(
                    out=Pp[k][:], in0=e[:, :, 128:256], in1=e[:, :, ec:ec + 1].to_broadcast([128, 8, 128]),
                    op=ALU.mult), [e], [Pp[k]])
                V(P, lambda: nc.vector.scalar_tensor_tensor(
                    out=Gp[k][:].rearrange("p a b -> p (a b)"), in0=Pp[k][:].rearrange("p a b -> p (a b)"),
                    scalar=1.0, in1=Pp[k][:].rearrange("p a b -> p (a b)"), op0=ALU.is_ge, op1=ALU.mult),
                  [Pp[k]], [Gp[k]])

            def stage_b(itm, k):
                eg, gi, ci = itm
                s = (eg % 2) * EG + ci
                dgi = Dg[gi]
                P.mm([lambda hh=hh: nc.tensor.matmul(pgt[k][:], lhsT=Gp[k][:, hh, :], rhs=dgi[:, hh, :],
                                                     start=(hh == 0), stop=(hh == 7))
                      for hh in range(8)], reads=[Gp[k], dgi], writes=[pgt[k]])
                V(P, lambda: nc.vector.tensor_tensor(out=GA[k][:], in0=pgt[k][:], in1=AbB[s][:, gi * 128:(gi + 1) * 128], op=ALU.mult),
                  [pgt[k], AbB[s]], [GA[k]])
                P.mm([lambda nb=nb: nc.tensor.matmul(
                    py[:, nb * 512:(nb + 1) * 512], lhsT=GA[k][:], rhs=vbb[s][:, nb * 512:(nb + 1) * 512],
                    start=(ci == 0), stop=(ci == EG - 1)) for nb in range(4)],
                     reads=[GA[k], vbb[s]], writes=[py])
                if ci == EG - 1:
                    if eg == 0:
                        A(P, lambda: nc.scalar.copy(out=yacc[gi][:], in_=py[:]), [py], [yacc[gi]])
                    else:
                        V(P, lambda: nc.vector.tensor_tensor(out=yacc[gi][:], in0=py[:], in1=yacc[gi][:], op=ALU.add),
                          [py, yacc[gi]], [yacc[gi]])

            for g_i, grp in enumerate(groups):
                first = (g_i == 0)
                ng = len(grp)
                for gi, t in enumerate(grp):
                    P.dma(hAll[:, :, gi * 128:(gi + 1) * 128], self.HT.ap[t].rearrange("p (a b) -> p a b", b=128),
                          reads=[self.HT.b(t)], writes=[hAll])
                    P.dma(Eg[gi][:].rearrange("p a b -> p (a b)"), self.EE.ap[t], reads=[self.EE.b(t)], writes=[Eg[gi]])
                    P.dma(Dg[gi][:].rearrange("p a b -> p (a b)"), self.DGS.ap[t], reads=[self.DGS.b(t)], writes=[Dg[gi]])
                load_eg(0, first)
                for ci in range(EG):
                    a0(0, ci, ng)
                load_eg(1, first)
                items = [(eg, gi, ci) for eg in range(NEG_) for gi in range(ng) for ci in range(EG)]
                stage_a(items[0], 0)
                for n, itm in enumerate(items):
                    eg, gi, ci = itm
                    if n + 1 < len(items):
                        stage_a(items[n + 1], (n + 1) % 2)
                    stage_b(itm, n % 2)
                    if ci == EG - 1:
                        if eg + 1 < NEG_:
                            todo = [c2 for c2 in range(EG) if (c2 % ng) == gi]
                            for c2 in todo:
                                a0(eg + 1, c2, ng)
                        if gi == ng - 1 and eg + 2 < NEG_:
                            load_eg(eg + 2, first)
                for gi, t in enumerate(grp):
                    P.dma(self.YP.ap[t], yacc[gi][:], reads=[yacc[gi]], writes=[self.YP.b(t)])
            P.barrier()
        with ExitStack() as pes:
            P.cur = pes
            pn = self.postnorm_alloc(li, 1, 5)
            ys = [P.sb([128, D], F32) for _ in range(2)]
            for i, t in enumerate(tiles):
                y = ys[i % 2]
                P.dma(y[:], self.YP.ap[t], reads=[self.YP.b(t)], writes=[y])
                self.postnorm_tile(pn, t, y, y[:])
            P.barrier()
        P.cur = P.es

    def key_tiles(self, ctx_q):
        kt = [(0, self.TL), (1, self.TL)]
        if not ctx_q:
            kt += [(r, t) for r in range(4) for t in range(self.TL)]
        return kt

    def q_groups(self):
        gs = []
        t = 0
        while t < self.TL:
            n = min(4, self.TL - t)
            gs.append((t * 128, n * 128, False))
            t += n
        gs.append((self.TL * 128, 128, True))
        return gs

    def rope_fm(self, raw, praw, prot, Qout, nq, q0, CS, tmp1, tmp2):
        P, nc = self.P, self.nc
        A(P, lambda: nc.scalar.copy(out=raw[0:64, 0:nq], in_=praw[0:64, 0:nq]), [praw], [raw])
        P.mm([lambda: nc.tensor.matmul(prot[0:64, 0:nq], lhsT=self.Rm[0:64, 0:64], rhs=raw[0:64, 0:nq], start=True, stop=True)],
             reads=[self.Rm, raw], writes=[prot])
        V(P, lambda: nc.vector.tensor_tensor(out=tmp1[0:64, 0:nq], in0=raw[0:64, 0:nq], in1=CS[0:64, 0, q0:q0 + nq], op=ALU.mult),
          [raw, CS], [tmp1])
        V(P, lambda: nc.vector.tensor_tensor(out=tmp2[0:64, 0:nq], in0=prot[0:64, 0:nq], in1=CS[0:64, 1, q0:q0 + nq], op=ALU.mult),
          [prot, CS], [tmp2])
        V(P, lambda: nc.vector.tensor_tensor(out=Qout[0:64, 0:nq], in0=tmp1[0:64, 0:nq], in1=tmp2[0:64, 0:nq], op=ALU.add),
          [tmp1, tmp2], [Qout])

    def load_rope_consts(self):
        P, nc = self.P, self.nc
        self.Rm = P.sb([64, 64], F32)
        P.dma(self.Rm[:], self.rmat.ap[:, :], reads=[self.rmat.b()], writes=[self.Rm])
        CS = P.sb([64, 2, self.NTOK], F32)
        P.dma(CS[:], self.ropeF.ap.rearrange("a d t -> d a t"), reads=[self.ropeF.b()], writes=[CS])
        return CS

    def gather(self, locs, alls):
        P, nc = self.P, self.nc
        P.barrier()
        si = self.cc_sem
        P._wait('pool', {k: v for k, v in enumerate(P.semval) if v > 0})
        for loc, allt in zip(locs, alls):
            ins = nc.gpsimd.collective_compute("AllGather", ALU.bypass, replica_groups=[[0, 1, 2, 3], [4, 5, 6, 7]],
                                               ins=[loc.th.ap().opt()], outs=[allt.th.ap().opt()])
            P.semval[si] += 1
            ins.then_inc(P.sems[si])
        P.barrier()

    def kv1_chunks(self):
        return [(t0, min(4, self.NTI - t0)) for t0 in range(0, self.NTI, 4)]

    def mla_pre(self, li):
        P, nc = self.P, self.nc
        tiles = self.tiles_of_layer(li)
        self.ln_mod_pass(li, 0, 1, self.HT)
        with ExitStack() as pes:
            P.cur = pes
            NW = 1088
            Win = P.sb([128, KC, NW], BF16)
            stg = [P.sb([128, D], F32) for _ in range(2)]
            self.load_w_bf16(Win, self.mla_w_in.ap[0], self.mla_w_in.b(), D, NW, stg)
            QN = P.sb([128, 512], F32)
            KVN = P.sb([128, 512], F32)
            self.bcast_tile(QN, self.mla_q_norm, self.mla_q_norm.ap[0, :])
            self.bcast_tile(KVN, self.mla_kv_norm, self.mla_kv_norm.ap[0, :])
            hT = [P.sb([128, KC, 128], BF16) for _ in range(2)]
            csb = P.sb([128, NW], F32)
            st = P.sb([128, 2, 6], F32)
            mv = P.sb([128, 2, 2], F32)
            ms = P.sb([128, 2], F32)
            nb_ = [P.sb([128, 512], BF16) for _ in range(2)]
            tab = [P.sb([128, 64], F32) for _ in range(2)]
            T1 = P.sb([128, 64], F32)
            T2 = P.sb([128, 64], F32)
            krr = P.sb([128, 64], F32)
            krb = P.sb([128, 64], BF16)
            cqT = [P.sb([128, 4, 128], BF16) for _ in range(2)]
            kvl = [P.sb([128, 5, 128], BF16) for _ in range(2)]
            for k in range(2):
                G(P, lambda k=k: nc.gpsimd.memset(kvl[k][:], 0.0), [], [kvl[k]])
            pc = P.ps([128, 1536], F32)
            pT = P.ps([128, 1152], BF16)
            blocks = [(0, 512), (512, 512), (1024, 64)]
            for i, t in enumerate(tiles):
                k = i % 2
                h = hT[k]
                P.dma(h[:].rearrange("p a b -> p (a b)"), self.HT.ap[t], reads=[self.HT.b(t)], writes=[h])
                if t < self.TL:
                    P.dma(tab[k][:], self.ropeT.ap[t], reads=[self.ropeT.b()], writes=[tab[k]])
                P.mm([lambda kc=kc, c0=c0, n=n, h=h: nc.tensor.matmul(pc[:, c0:c0 + n], lhsT=h[:, kc, :], rhs=Win[:, kc, c0:c0 + n],
                                                                     start=(kc == 0), stop=(kc == KC - 1))
                      for (c0, n) in blocks for kc in range(KC)], reads=[h, Win], writes=[pc])
                A(P, lambda: nc.scalar.copy(out=csb[:], in_=pc[:, 0:NW]), [pc], [csb])
                if CUT <= 1:
                    continue
                for j in range(2):
                    V(P, lambda j=j: nc.vector.bn_stats(out=st[:, j, :], in_=csb[:, j * 512:(j + 1) * 512]), [csb], [st])
                    V(P, lambda j=j: nc.vector.bn_aggr(out=mv[:, j, :], in_=st[:, j:j + 1, :]), [st], [mv])
                V(P, lambda: nc.vector.tensor_tensor(out=ms[:], in0=mv[:, :, 0], in1=mv[:, :, 0], op=ALU.mult), [mv], [ms])
                V(P, lambda: nc.vector.tensor_tensor(out=ms[:], in0=ms[:], in1=mv[:, :, 1], op=ALU.add), [ms, mv], [ms])
                A(P, lambda: nc.scalar.activation(out=ms[:], in_=ms[:], func=AF.Sqrt, bias=self.epsb[:, 0:1], scale=1.0), [ms, self.epsb], [ms])
                V(P, lambda: nc.vector.reciprocal(out=ms[:], in_=ms[:]), [ms], [ms])
                for j, NT_ in enumerate((QN, KVN)):
                    V(P, lambda j=j, NT_=NT_: nc.vector.scalar_tensor_tensor(out=nb_[j][:], in0=csb[:, j * 512:(j + 1) * 512],
                                                                            scalar=ms[:, j:j + 1], in1=NT_[:], op0=ALU.mult, op1=ALU.mult),
                      [csb, ms, NT_], [nb_[j]])
                if CUT <= 2:
                    continue
                if t < self.TL:
                    kv4 = csb[:, 1024:1088].rearrange("p (a b d) -> p a b d", a=2, b=2)
                    tb4 = tab[k][:].rearrange("p (a b d) -> p a b d", a=2, b=2)
                    t14 = T1[:].rearrange("p (a b d) -> p a b d", a=2, b=2)
                    t24 = T2[:].rearrange("p (a b d) -> p a b d", a=2, b=2)
                    o4 = krr[:].rearrange("p (a b d) -> p a b d", a=2, b=2)
                    tk = tab[k]
                    V(P, lambda: nc.vector.tensor_tensor(out=t14[:, :, 0, :], in0=kv4[:, :, 0, :], in1=tb4[:, :, 0, :], op=ALU.mult), [csb, tk], [T1])
                    V(P, lambda: nc.vector.tensor_tensor(out=t14[:, :, 1, :], in0=kv4[:, :, 1, :], in1=tb4[:, :, 1, :], op=ALU.mult), [csb, tk], [T1])
                    V(P, lambda: nc.vector.tensor_tensor(out=t24[:, :, 0, :], in0=kv4[:, :, 0, :], in1=tb4[:, :, 1, :], op=ALU.mult), [csb, tk], [T2])
                    V(P, lambda: nc.vector.tensor_tensor(out=t24[:, :, 1, :], in0=kv4[:, :, 1, :], in1=tb4[:, :, 0, :], op=ALU.mult), [csb, tk], [T2])
                    V(P, lambda: nc.vector.tensor_tensor(out=o4[:, :, 0, :], in0=t14[:, :, 0, :], in1=t14[:, :, 1, :], op=ALU.subtract), [T1], [krr])
                    V(P, lambda: nc.vector.tensor_tensor(out=o4[:, :, 1, :], in0=t24[:, :, 0, :], in1=t24[:, :, 1, :], op=ALU.add), [T2], [krr])
                    V(P, lambda: nc.vector.tensor_copy(out=krb[:], in_=krr[:]), [krr], [krb])
                else:
                    V(P, lambda: nc.vector.tensor_copy(out=krb[:], in_=csb[:, 1024:1088]), [csb], [krb])
                if CUT <= 3:
                    continue
                P.mm([lambda j=j, c=c: nc.tensor.transpose(out=pT[:, (j * 4 + c) * 128:(j * 4 + c + 1) * 128],
                                                           in_=nb_[j][:, c * 128:(c + 1) * 128], identity=self.ident[:])
                      for j in range(2) for c in range(4)] +
                     [lambda: nc.tensor.transpose(out=pT[0:64, 1024:1152], in_=krb[:, 0:64], identity=self.ident[:])],
                     reads=[nb_[0], nb_[1], krb, self.ident], writes=[pT])
                if CUT <= 4:
                    continue
                A(P, lambda k=k: nc.scalar.copy(out=cqT[k][:].rearrange("p a b -> p (a b)"), in_=pT[:, 0:512]), [pT], [cqT[k]])
                A(P, lambda k=k: nc.scalar.copy(out=kvl[k][:, 0:4, :].rearrange("p a b -> p (a b)"), in_=pT[:, 512:1024]), [pT], [kvl[k]])
                A(P, lambda k=k: nc.scalar.copy(out=kvl[k][0:64, 4, :], in_=pT[0:64, 1024:1152]), [pT], [kvl[k]])
                if CUT <= 5:
                    continue
                P.dma(self.CQT.ap[t * 128:(t + 1) * 128, :], cqT[k][:].rearrange("p a b -> p (a b)"), reads=[cqT[k]], writes=[self.CQT.b(('w', t))])
                if CUT <= 6:
                    continue
                P.dma(self.KV1_loc[t // 4].ap[(t % 4) * 128:(t % 4 + 1) * 128, :], kvl[k][:].rearrange("p a b -> p (a b)"), reads=[kvl[k]],
                      writes=[self.KV1_loc[t // 4].b(('w', t))])
            P.barrier()
        P.cur = P.es

    def mla_attn(self, li):
        P, nc = self.P, self.nc
        TL, NTI, NTOK = self.TL, self.NTI, self.NTOK
        with ExitStack() as pes:
            P.cur = pes
            CS = self.load_rope_consts()
            ktl = self.key_tiles(False)
            NKT = len(ktl)
            NK = NKT * 128
            KVall = P.sb([128, 5, NK], BF16)
            for kt, (r, t) in enumerate(ktl):
                ch = self.kv1_chunks()[t // 4]
                row = (r * ch[1] + (t % 4)) * 128
                P.dma(KVall[:, :, kt * 128:(kt + 1) * 128], self.KV1_all[t // 4].ap[row:row + 128, :].rearrange("p (a b) -> p a b", b=128),
                      reads=[self.KV1_all[t // 4].b()], writes=[KVall])
            cq = P.sb([128, 4, NTOK], BF16)
            for t in range(NTI):
                P.dma(cq[:, :, t * 128:(t + 1) * 128], self.CQT.ap[t * 128:(t + 1) * 128, :].rearrange("p (a b) -> p a b", b=128),
                      reads=[self.CQT.b()], writes=[cq])
            wkf = P.sb([128, 4, 256], F32)
            wqf = P.sb([128, 4, 192], F32)
            Wk = P.sb([128, 4, 256], BF16)
            Wq = P.sb([128, 4, 192], BF16)
            Kh = P.sb([128, NK], BF16)
            Vh = P.sb([128, NKT, 128], BF16)
            Qn = P.sb([128, 512], BF16)
            Qr = P.sb([64, 512], BF16)
            raw = P.sb([64, 512], F32)
            tmp1 = P.sb([64, 512], F32)
            tmp2 = P.sb([64, 512], F32)
            PT = [P.sb([128, 512], BF16) for _ in range(2)]
            rL = P.sb([128, 512], F32)
            ao = [P.sb([128, 512], BF16) for _ in range(2)]
            pS = [P.ps([128, 512], F32) for _ in range(2)]
            pO = P.ps([128, 512], F32)
            pL = P.ps([128, 512], F32)
            pX = [P.ps([128, 512], F32) for _ in range(2)]
            wukv = self.mla_w_ukv.ap[0].rearrange("(c p) n -> p c n", p=128)
            wuq = self.mla_w_uq.ap[0].rearrange("(c p) n -> p c n", p=128)
            nao = 0
            for h in range(16):
                P.dma(wkf[:], wukv[:, :, h * 256:(h + 1) * 256], reads=[self.mla_w_ukv.b()], writes=[wkf])
                P.dma(wqf[:], wuq[:, :, h * 192:(h + 1) * 192], reads=[self.mla_w_uq.b()], writes=[wqf])
                G(P, lambda: nc.gpsimd.tensor_copy(out=Wk[:], in_=wkf[:]), [wkf], [Wk])
                G(P, lambda: nc.gpsimd.tensor_copy(out=Wq[:], in_=wqf[:]), [wqf], [Wq])
                xi = 0
                for k0 in range(0, NK, 512):
                    n = min(512, NK - k0)
                    px = pX[xi % 2]
                    xi += 1
                    P.mm([lambda c=c, px=px, k0=k0, n=n: nc.tensor.matmul(px[:, 0:n], lhsT=Wk[:, c, 0:128], rhs=KVall[:, c, k0:k0 + n],
                                                                         start=(c == 0), stop=(c == 3)) for c in range(4)],
                         reads=[Wk, KVall], writes=[px])
                    V(P, lambda px=px, k0=k0, n=n: nc.vector.tensor_copy(out=Kh[:, k0:k0 + n], in_=px[:, 0:n]), [px], [Kh])
                for kt0 in range(0, NKT, 4):
                    n = min(4, NKT - kt0)
                    px = pX[xi % 2]
                    xi += 1
                    P.mm([lambda c=c, j=j, px=px, kt0=kt0: nc.tensor.matmul(px[:, j * 128:(j + 1) * 128],
                                                                           lhsT=KVall[:, c, (kt0 + j) * 128:(kt0 + j + 1) * 128],
                                                                           rhs=Wk[:, c, 128:256], start=(c == 0), stop=(c == 3))
                          for j in range(n) for c in range(4)], reads=[Wk, KVall], writes=[px])
                    V(P, lambda px=px, kt0=kt0, n=n: nc.vector.tensor_copy(out=Vh[:, kt0:kt0 + n, :].rearrange("p a b -> p (a b)"),
                                                                          in_=px[:, 0:n * 128]), [px], [Vh])
                for (q0, nq, isctx) in self.q_groups():
                    px = pX[xi % 2]
                    xi += 1
                    P.mm([lambda c=c, px=px: nc.tensor.matmul(px[:, 0:nq], lhsT=Wq[:, c, 0:128], rhs=cq[:, c, q0:q0 + nq],
                                                              start=(c == 0), stop=(c == 3)) for c in range(4)],
                         reads=[Wq, cq], writes=[px])
                    V(P, lambda px=px: nc.vector.tensor_copy(out=Qn[:, 0:nq], in_=px[:, 0:nq]), [px], [Qn])
                    px2 = pX[xi % 2]
                    xi += 1
                    P.mm([lambda c=c, px2=px2: nc.tensor.matmul(px2[0:64, 0:nq], lhsT=Wq[:, c, 128:192], rhs=cq[:, c, q0:q0 + nq],
                                                                start=(c == 0), stop=(c == 3)) for c in range(4)],
                         reads=[Wq, cq], writes=[px2])
                    if isctx:
                        V(P, lambda px2=px2: nc.vector.tensor_copy(out=Qr[0:64, 0:nq], in_=px2[0:64, 0:nq]), [px2], [Qr])
                    else:
                        px3 = pX[xi % 2]
                        xi += 1
                        self.rope_fm(raw, px2, px3, Qr, nq, q0, CS, tmp1, tmp2)
                    kts = list(range(2)) if isctx else list(range(NKT))

                    def s_stage(kt, k2):
                        P.mm([lambda: nc.tensor.matmul(pS[k2][:, 0:nq], lhsT=Kh[:, kt * 128:(kt + 1) * 128], rhs=Qn[:, 0:nq],
                                                       start=True, stop=False),
                              lambda: nc.tensor.matmul(pS[k2][:, 0:nq], lhsT=KVall[0:64, 4, kt * 128:(kt + 1) * 128], rhs=Qr[0:64, 0:nq],
                                                       start=False, stop=True)],
                             reads=[Kh, Qn, KVall, Qr], writes=[pS[k2]])
                    s_stage(kts[0], 0)
                    for ii, kt in enumerate(kts):
                        k2 = ii % 2
                        if ii + 1 < len(kts):
                            s_stage(kts[ii + 1], (ii + 1) % 2)
                        A(P, lambda k2=k2: nc.scalar.activation(out=PT[k2][:, 0:nq], in_=pS[k2][:, 0:nq], func=AF.Exp, scale=MLA_SCALE),
                          [pS[k2]], [PT[k2]])
                        first, last = (ii == 0), (ii == len(kts) - 1)
                        P.mm([lambda kt=kt, k2=k2: nc.tensor.matmul(pO[:, 0:nq], lhsT=Vh[:, kt, :], rhs=PT[k2][:, 0:nq], start=first, stop=last),
                              lambda k2=k2: nc.tensor.matmul(pL[:, 0:nq], lhsT=self.onesb[:], rhs=PT[k2][:, 0:nq], start=first, stop=last)],
                             reads=[Vh, PT[k2], self.onesb], writes=[pO, pL])
                    V(P, lambda: nc.vector.reciprocal(out=rL[:, 0:nq], in_=pL[:, 0:nq]), [pL], [rL])
                    a = ao[nao % 2]
                    nao += 1
                    V(P, lambda a=a: nc.vector.tensor_tensor(out=a[:, 0:nq], in0=pO[:, 0:nq], in1=rL[:, 0:nq], op=ALU.mult), [pO, rL], [a])
                    P.dma(self.AOT.ap[h, :, q0:q0 + nq], a[:, 0:nq], reads=[a], writes=[self.AOT.b(('w', h, q0))])
            P.barrier()
        P.cur = P.es

    def da_pre(self, li):
        P, nc = self.P, self.nc
        TL, NTI, NTOK = self.TL, self.NTI, self.NTOK
        tiles = self.tiles_of_layer(li)
        self.ln_mod_pass(li, 0, 1, self.HT)
        with ExitStack() as pes:
            P.cur = pes
            Wv = P.sb([128, KC, D], BF16)
            stg = [P.sb([128, D], F32) for _ in range(2)]
            self.load_w_bf16(Wv, self.da_w_in.ap[0], self.da_w_in.b(), D, D, stg, c0=2 * D)
            hT = [P.sb([128, KC, 128], BF16) for _ in range(2)]
            vb = [P.sb([128, D], BF16) for _ in range(2)]
            pv = P.ps([128, D], F32)
            for i, t in enumerate(tiles):
                k = i % 2
                h = hT[k]
                P.dma(h[:].rearrange("p a b -> p (a b)"), self.HT.ap[t], reads=[self.HT.b(t)], writes=[h])
                P.mm([lambda kc=kc, nb=nb, h=h: nc.tensor.matmul(pv[:, nb * 512:(nb + 1) * 512], lhsT=h[:, kc, :],
                                                                 rhs=Wv[:, kc, nb * 512:(nb + 1) * 512],
                                                                 start=(kc == 0), stop=(kc == KC - 1))
                      for nb in range(4) for kc in range(KC)], reads=[h, Wv], writes=[pv])
                A(P, lambda k=k: nc.scalar.copy(out=vb[k][:], in_=pv[:]), [pv], [vb[k]])
                for hh in range(16):
                    P.dma(self.V_loc[hh].ap[:, t * 128:(t + 1) * 128], vb[k][:, hh * 128:(hh + 1) * 128], reads=[vb[k]],
                          writes=[self.V_loc[hh].b(('w', t))])
            P.barrier()
        for which in ("k", "q"):
            with ExitStack() as pes:
                P.cur = pes
                CS = self.load_rope_consts()
                Ww = P.sb([128, KC, D], BF16)
                stg = [P.sb([128, D], F32) for _ in range(2)]
                self.load_w_bf16(Ww, self.da_w_in.ap[0], self.da_w_in.b(), D, D, stg, c0=(D if which == "k" else 0))
                hA = P.sb([128, KC, NTOK], BF16)
                for t in tiles:
                    P.dma(hA[:, :, t * 128:(t + 1) * 128], self.HT.ap[t].rearrange("p (a b) -> p a b", b=128), reads=[self.HT.b(t)], writes=[hA])
                raw = P.sb([64, 512], F32)
                tmp1 = P.sb([64, 512], F32)
                tmp2 = P.sb([64, 512], F32)
                ob = [P.sb([64, 512], BF16) for _ in range(2)]
                pr = [P.ps([128, 512], F32) for _ in range(2)]
                prot = [P.ps([128, 512], F32) for _ in range(2)]
                n_o = 0
                for hc in range(32):
                    for (q0, nq, isctx) in self.q_groups():
                        k = n_o % 2
                        n_o += 1
                        P.mm([lambda kc=kc, k=k: nc.tensor.matmul(pr[k][0:64, 0:nq], lhsT=Ww[:, kc, hc * 64:(hc + 1) * 64],
                                                                   rhs=hA[:, kc, q0:q0 + nq], start=(kc == 0), stop=(kc == KC - 1))
                              for kc in range(KC)], reads=[Ww, hA], writes=[pr[k]])
                        if isctx:
                            V(P, lambda k=k: nc.vector.tensor_copy(out=ob[k][0:64, 0:nq], in_=pr[k][0:64, 0:nq]), [pr[k]], [ob[k]])
                        else:
                            self.rope_fm(raw, pr[k], prot[k], ob[k], nq, q0, CS, tmp1, tmp2)
                        if which == "k":
                            dk = self.KT_loc[hc // 2]
                            P.dma(dk.ap[(hc % 2) * 64:(hc % 2 + 1) * 64, q0:q0 + nq], ob[k][0:64, 0:nq], reads=[ob[k]], writes=[dk.b(('w', hc, q0))])
                        else:
                            P.dma(self.QT.ap[hc * 64:(hc + 1) * 64, q0:q0 + nq], ob[k][0:64, 0:nq], reads=[ob[k]],
                                  writes=[self.QT.b(('w', hc, q0))])
                P.barrier()
        P.cur = P.es

    def da_attn(self, li):
        P, nc = self.P, self.nc
        TL, NTI, NTOK = self.TL, self.NTI, self.NTOK
        lam_init = 0.8 - 0.6 * math.exp(-0.3 * li)
        with ExitStack() as pes:
            P.cur = pes
            lp = P.sb([1, 4, 64], F32)
            pr2 = P.sb([1, 2, 64], F32)
            s2 = P.sb([1, 2], F32)
            lam1 = P.sb([1, 1], F32)
            NEGLAM = P.sb([128, 1], F32)
            SUBL = P.sb([128, 1], F32)
            pS = [[P.ps([128, 512], F32) for _ in range(2)] for _ in range(2)]
            pl = pS[0][0]
            P.dma(lp[:], self.da_lambda.ap[0:1, :, :], reads=[self.da_lambda.b()], writes=[lp])
            lp4 = lp[:].rearrange("p (a b) d -> p a b d", b=2)
            V(P, lambda: nc.vector.tensor_tensor(out=pr2[:], in0=lp4[:, :, 0, :], in1=lp4[:, :, 1, :], op=ALU.mult), [lp], [pr2])
            V(P, lambda: nc.vector.tensor_reduce(out=s2[:], in_=pr2[:], axis=mybir.AxisListType.X, op=ALU.add), [pr2], [s2])
            A(P, lambda: nc.scalar.activation(out=s2[:], in_=s2[:], func=AF.Exp), [s2], [s2])
            V(P, lambda: nc.vector.scalar_tensor_tensor(out=lam1[:], in0=s2[:, 1:2], scalar=-lam_init, in1=s2[:, 0:1],
                                                        op0=ALU.add, op1=ALU.subtract), [s2], [lam1])
            P.mm([lambda: nc.tensor.matmul(pl[:, 0:1], lhsT=self.onesf[0:1, :], rhs=lam1[:], start=True, stop=True)],
                 reads=[self.onesf, lam1], writes=[pl])
            V(P, lambda: nc.vector.tensor_copy(out=NEGLAM[:], in_=pl[:, 0:1]), [pl], [NEGLAM])
            P.dma(SUBL[:], self.da_subln.ap[0, :].rearrange("(p o) -> p o", o=1), reads=[self.da_subln.b()], writes=[SUBL])
            V(P, lambda: nc.vector.tensor_scalar(out=SUBL[:], in0=SUBL[:], scalar1=(1.0 - lam_init), scalar2=None, op0=ALU.mult), [SUBL], [SUBL])
            ktl = self.key_tiles(False)
            NKT = len(ktl)
            NK = NKT * 128
            K12 = [P.sb([64, NK], BF16) for _ in range(2)]
            Vh = P.sb([128, NKT, 128], BF16)
            Q12 = [P.sb([64, NTOK], BF16) for _ in range(2)]
            PT = [[P.sb([128, 512], BF16) for _ in range(2)] for _ in range(2)]
            r1 = P.sb([128, 512], F32)
            oa = P.sb([128, 512], F32)
            ob = P.sb([128, 512], F32)
            sq = P.sb([128, 512], F32)
            ao = [P.sb([128, 512], BF16) for _ in range(2)]
            pO = [P.ps([128, 512], F32) for _ in range(2)]
            pL = [P.ps([128, 512], F32) for _ in range(2)]
            nao = 0
            for h in range(16):
                for cmp_ in range(2):
                    hc = 2 * h + cmp_
                    Kc = K12[cmp_]
                    kall = self.KT_all[h]
                    for r in range(2):
                        P.dma(Kc[:, r * 128:(r + 1) * 128], kall.ap[r * 128 + cmp_ * 64:r * 128 + (cmp_ + 1) * 64, TL * 128:NTI * 128],
                              reads=[kall.b()], writes=[Kc])
                    for r in range(4):
                        P.dma(Kc[:, (2 + r * TL) * 128:(2 + (r + 1) * TL) * 128], kall.ap[r * 128 + cmp_ * 64:r * 128 + (cmp_ + 1) * 64, 0:TL * 128],
                              reads=[kall.b()], writes=[Kc])
                    P.dma(Q12[cmp_][:], self.QT.ap[hc * 64:(hc + 1) * 64, :], reads=[self.QT.b()], writes=[Q12[cmp_]])
                vall = self.V_all[h]
                for r in range(2):
                    P.dma(Vh[:, r, :], vall.ap[r * 128:(r + 1) * 128, TL * 128:NTI * 128], reads=[vall.b()], writes=[Vh])
                for r in range(4):
                    P.dma(Vh[:, 2 + r * TL:2 + (r + 1) * TL, :].rearrange("p a b -> p (a b)"),
                          vall.ap[r * 128:(r + 1) * 128, 0:TL * 128], reads=[vall.b()], writes=[Vh])
                for (q0, nq, isctx) in self.q_groups():
                    kts = list(range(2)) if isctx else list(range(NKT))

                    def s_stage(kt, k2):
                        for cmp_ in range(2):
                            P.mm([lambda cmp_=cmp_: nc.tensor.matmul(pS[cmp_][k2][:, 0:nq], lhsT=K12[cmp_][:, kt * 128:(kt + 1) * 128],
                                                                      rhs=Q12[cmp_][:, q0:q0 + nq], start=True, stop=True)],
                                 reads=[K12[cmp_], Q12[cmp_]], writes=[pS[cmp_][k2]])
                    s_stage(kts[0], 0)
                    for ii, kt in enumerate(kts):
                        k2 = ii % 2
                        if ii + 1 < len(kts):
                            s_stage(kts[ii + 1], (ii + 1) % 2)
                        first, last = (ii == 0), (ii == len(kts) - 1)
                        for cmp_ in range(2):
                            A(P, lambda cmp_=cmp_, k2=k2: nc.scalar.activation(out=PT[cmp_][k2][:, 0:nq], in_=pS[cmp_][k2][:, 0:nq],
                                                                               func=AF.Exp, scale=DA_SCALE), [pS[cmp_][k2]], [PT[cmp_][k2]])
                            P.mm([lambda cmp_=cmp_, kt=kt, k2=k2: nc.tensor.matmul(pO[cmp_][:, 0:nq], lhsT=Vh[:, kt, :], rhs=PT[cmp_][k2][:, 0:nq],
                                                                                   start=first, stop=last),
                                  lambda cmp_=cmp_, k2=k2: nc.tensor.matmul(pL[cmp_][:, 0:nq], lhsT=self.onesb[:], rhs=PT[cmp_][k2][:, 0:nq],
                                                                            start=first, stop=last)],
                                 reads=[Vh, PT[cmp_][k2], self.onesb], writes=[pO[cmp_], pL[cmp_]])
                    V(P, lambda: nc.vector.reciprocal(out=r1[:, 0:nq], in_=pL[0][:, 0:nq]), [pL[0]], [r1])
                    V(P, lambda: nc.vector.tensor_tensor(out=oa[:, 0:nq], in0=pO[0][:, 0:nq], in1=r1[:, 0:nq], op=ALU.mult), [pO[0], r1], [oa])
                    V(P, lambda: nc.vector.reciprocal(out=r1[:, 0:nq], in_=pL[1][:, 0:nq]), [pL[1]], [r1])
                    V(P, lambda: nc.vector.tensor_tensor(out=ob[:, 0:nq], in0=pO[1][:, 0:nq], in1=r1[:, 0:nq], op=ALU.mult), [pO[1], r1], [ob])
                    V(P, lambda: nc.vector.scalar_tensor_tensor(out=oa[:, 0:nq], in0=ob[:, 0:nq], scalar=NEGLAM[:, 0:1], in1=oa[:, 0:nq],
                                                                op0=ALU.mult, op1=ALU.add), [ob, NEGLAM, oa], [oa])
                    V(P, lambda: nc.vector.tensor_tensor(out=sq[:, 0:nq], in0=oa[:, 0:nq], in1=oa[:, 0:nq], op=ALU.mult), [oa], [sq])
                    pss = pS[0][0]
                    P.mm([lambda: nc.tensor.matmul(pss[:, 0:nq], lhsT=self.onesf[:], rhs=sq[:, 0:nq], start=True, stop=True)],
                         reads=[self.onesf, sq], writes=[pss])
                    A(P, lambda: nc.scalar.activation(out=r1[:, 0:nq], in_=pss[:, 0:nq], func=AF.Sqrt, bias=self.epsb[:, 0:1], scale=1.0 / 128.0),
                      [pss, self.epsb], [r1])
                    V(P, lambda: nc.vector.reciprocal(out=r1[:, 0:nq], in_=r1[:, 0:nq]), [r1], [r1])
                    V(P, lambda: nc.vector.tensor_tensor(out=oa[:, 0:nq], in0=oa[:, 0:nq], in1=r1[:, 0:nq], op=ALU.mult), [oa, r1], [oa])
                    a = ao[nao % 2]
                    nao += 1
                    V(P, lambda a=a: nc.vector.tensor_scalar(out=a[:, 0:nq], in0=oa[:, 0:nq], scalar1=SUBL[:, 0:1], scalar2=None, op0=ALU.mult),
                      [oa, SUBL], [a])
                    P.dma(self.AOT.ap[h, :, q0:q0 + nq], a[:, 0:nq], reads=[a], writes=[self.AOT.b(('w', h, q0))])
            P.barrier()
        P.cur = P.es

    def attn_out(self, li, w_dbuf):
        self.outproj_pass(li, w_dbuf.ap[0], w_dbuf.b(), self.AOT,
                          lambda t: self.AOT.ap[:, :, t * 128:(t + 1) * 128].rearrange("h p t -> p h t"))

    ALL_STEPS = [
        [('mod', 0), ('gmlp', 0), ('peer', 0), ('mod', 1), ('mla_pre', 1)],
        [('mod', 1), ('mla_attn', 1), ('mla_out', 1), ('peer', 1), ('mod', 2), ('da_pre', 2)],
        [('mod', 2), ('da_attn', 2), ('da_out', 2), ('peer', 2), ('mod', 3), ('gmlp', 3), ('peer', 3)],
    ]

    def plan(self):
        ph = self.phase
        if ph == 'all':
            steps = self.ALL_STEPS[0] + [('gather_mla', 1)] + self.ALL_STEPS[1] + [('gather_da', 2)] + self.ALL_STEPS[2]
        else:
            steps = list(self.ALL_STEPS[ph])
        steps = [st for st in steps if st[1] in self.layers and st[0] not in self.skip]
        self.steps = steps
        self.ML = sorted({l for (k, l) in steps if k == 'mod'})
        self.PL = sorted({l for (k, l) in steps if k == 'peer'})
        self.GL = sorted({l for (k, l) in steps if k == 'gmlp'})
        self.midx = {l: i for i, l in enumerate(self.ML)}
        self.lidx = {l: i for i, l in enumerate(self.PL)}
        self.gidx = {l: i for i, l in enumerate(self.GL)}
        self.kinds = {k for (k, l) in steps}

    def build(self):
        nc = self.nc
        TL, NTI, NTOK = self.TL, self.NTI, self.NTOK
        self.plan()
        kinds = self.kinds
        self.xin = self.din("xin", [NTI, 128, D])
        self.cT = self.din("cT", [128, KC, 2])
        self.mod_w = self.din("mod_w", [max(1, len(self.ML)), D, 6 * D])
        self.mod_b = self.din("mod_b", [DEPTH, 6 * D])
        self.ln_g = self.din("ln_g", [DEPTH, 2, D])
        self.ln_b = self.din("ln_b", [DEPTH, 2, D])
        if self.PL:
            npl = len(self.PL)
            self.peer_wq = self.din("peer_wq", [npl, D, D])
            self.peer_k1T = self.din("peer_k1T", [npl, 128, 128])
            self.peer_k2T = self.din("peer_k2T", [npl, 128, 128])
            self.peer_uL = self.din("peer_uL", [npl, 128, 128, KC, 128])
            self.peer_v = self.din("peer_v", [npl, 16384, D])
            self.EE = self.dscr("EE", [NTI, 128, D])
            self.DGS = self.dscr("DGS", [NTI, 128, 1024], BF16)
            self.YP = self.dscr("YP", [NTI, 128, D])
            self.UB = self.dscr("UB", [128, 128, D], BF16)
            self.VB = self.dscr("VB", [128, 128, D], BF16)
        if self.GL:
            ng = len(self.GL)
            self.gm_w_in = self.din("gm_w_in", [ng, D, 2 * D])
            self.gm_ln_g = self.din("gm_ln_g", [ng, D])
            self.gm_ln_b = self.din("gm_ln_b", [ng, D])
            self.gm_wsT = self.din("gm_wsT", [ng, 16, 128, 128])
            self.gm_bs = self.din("gm_bs", [ng, 16, 128])
            self.gm_w_out = self.din("gm_w_out", [ng, D, D])
            self.VN = self.dscr("VN", [NTI, 128, D], BF16)
            self.MT = self.dscr("MT", [NTI, 128, D], BF16)
        if kinds & {'mla_pre', 'mla_attn', 'da_pre', 'da_attn'}:
            self.ropeT = self.din("ropeT", [NTI, 128, 64])
            self.ropeF = self.din("ropeF", [2, 64, NTOK])
            self.rmat = self.din("rmat", [64, 64])
            self.AOT = self.dscr("AOT", [16, 128, NTOK], BF16)
        if 'mla_pre' in kinds:
            self.mla_w_in = self.din("mla_w_in", [1, D, 1088])
            self.mla_q_norm = self.din("mla_q_norm", [1, 512])
            self.mla_kv_norm = self.din("mla_kv_norm", [1, 512])
        if 'mla_attn' in kinds:
            self.mla_w_uq = self.din("mla_w_uq", [1, 512, 3072])
            self.mla_w_ukv = self.din("mla_w_ukv", [1, 512, 4096])
        if 'mla_out' in kinds:
            self.mla_w_out = self.din("mla_w_out", [1, D, D])
        if kinds & {'da_pre'}:
            self.da_w_in = self.din("da_w_in", [1, D, 3 * D])
        if 'da_attn' in kinds:
            self.da_lambda = self.din("da_lambda", [1, 4, 64])
            self.da_subln = self.din("da_subln", [1, 128])
        if 'da_out' in kinds:
            self.da_w_out = self.din("da_w_out", [1, D, D])
        fused = (self.phase == 'all')
        if kinds & {'mla_pre', 'mla_attn'}:
            self.CQT = self.xphase("CQT", [NTOK, 512], BF16, 0, 1)
            mk_loc = self.dscr if fused else self.dout
            mk_all = self.dscr if fused else self.din
            if fused or 'mla_pre' in kinds:
                self.KV1_loc = [mk_loc("KV1_loc_%d" % j, [n * 128, 640], BF16) for j, (t0, n) in enumerate(self.kv1_chunks())]
            if fused or 'mla_attn' in kinds:
                self.KV1_all = [mk_all("KV1_all_%d" % j, [4 * n * 128, 640], BF16) for j, (t0, n) in enumerate(self.kv1_chunks())]
        if kinds & {'da_pre', 'da_attn'}:
            self.QT = self.xphase("QT", [D, NTOK], BF16, 1, 2)
            mk_loc = self.dscr if fused else self.dout
            mk_all = self.dscr if fused else self.din
            if fused or 'da_pre' in kinds:
                self.KT_loc = [mk_loc("KT_loc_%d" % hh, [128, NTOK], BF16) for hh in range(16)]
                self.V_loc = [mk_loc("V_loc_%d" % hh, [128, NTOK], BF16) for hh in range(16)]
            if fused or 'da_attn' in kinds:
                self.KT_all = [mk_all("KT_all_%d" % hh, [4 * 128, NTOK], BF16) for hh in range(16)]
                self.V_all = [mk_all("V_all_%d" % hh, [4 * 128, NTOK], BF16) for hh in range(16)]
        self.xout = self.dout("xout", [NTI, 128, D])
        self.X = self.dscr("X", [NTI, 128, D])
        self.MOD = self.dscr("MOD", [DEPTH, 2, 6 * D])
        self.HT = self.dscr("HT", [NTI, 128, D], BF16)
        with ExitStack() as es:
            self.P = P = Prog(nc, es)
            self.cc_sem = P.new_sem("cc")
            self.consts()
            with ExitStack() as pes:
                P.cur = pes
                xt = [P.sb([128, D], F32) for _ in range(2)]
                for t in range(NTI):
                    P.dma(xt[t % 2][:], self.xin.ap[t], reads=[self.xin.b()], writes=[xt[t % 2]])
                    P.dma(self.X.ap[t], xt[t % 2][:], reads=[xt[t % 2]], writes=[self.X.b(t)])
                P.barrier()
            P.cur = P.es
            for (kind, li) in self.steps:
                if kind == 'mod':
                    self.mod_pass(li)
                elif kind == 'gmlp':
                    self.gmlp(li, self.gidx[li])
                elif kind == 'peer':
                    self.peer(li)
                elif kind == 'mla_pre':
                    self.mla_pre(li)
                elif kind == 'gather_mla':
                    self.gather(self.KV1_loc, self.KV1_all)
                elif kind == 'mla_attn':
                    self.mla_attn(li)
                elif kind == 'mla_out':
                    self.attn_out(li, self.mla_w_out)
                elif kind == 'da_pre':
                    self.da_pre(li)
                elif kind == 'gather_da':
                    self.gather(self.KT_loc + self.V_loc, self.KT_all + self.V_all)
                elif kind == 'da_attn':
                    self.da_attn(li)
                elif kind == 'da_out':
                    self.attn_out(li, self.da_w_out)
            xt = [P.sb([128, D], F32) for _ in range(2)]
            for t in range(NTI):
                P.dma(xt[t % 2][:], self.X.ap[t], reads=[self.X.b(t)], writes=[xt[t % 2]])
                P.dma(self.xout.ap[t], xt[t % 2][:], reads=[xt[t % 2]], writes=[self.xout.b(t)])
            P.barrier()
        return nc


def _npdt(dt):
    return ml_dtypes.bfloat16 if dt == BF16 else np.float32


def prep_shared(inputs, bld):
    f = lambda a: np.ascontiguousarray(np.asarray(a, dtype=np.float32))
    need = bld.inputs
    sh = {}
    ML, PL, GL = bld.ML, bld.PL, [l // 3 for l in bld.GL]
    if "mod_w" in need:
        sh["mod_w"] = f(np.asarray(inputs["mod_w"])[ML or [0]])
    sh["mod_b"] = f(inputs["mod_b"])
    sh["ln_g"] = f(inputs["ln_g"])
    sh["ln_b"] = f(inputs["ln_b"])
    if PL:
        sh["peer_wq"] = f(np.asarray(inputs["peer_wq"])[PL])
        sh["peer_k1T"] = f(np.asarray(inputs["peer_k1"])[PL].transpose(0, 2, 1))
        sh["peer_k2T"] = f(np.asarray(inputs["peer_k2"])[PL].transpose(0, 2, 1))
        u = np.asarray(inputs["peer_u"])[PL]
        sh["peer_uL"] = f(u.reshape(len(PL), 128, 128, KC, 128).transpose(0, 1, 4, 3, 2))
        sh["peer_v"] = f(np.asarray(inputs["peer_v"])[PL])
    if GL:
        sh["gm_w_in"] = f(np.asarray(inputs["gm_w_in"])[GL])
        sh["gm_ln_g"] = f(np.asarray(inputs["gm_ln_g"])[GL])
        sh["gm_ln_b"] = f(np.asarray(inputs["gm_ln_b"])[GL])
        sh["gm_wsT"] = f(np.asarray(inputs["gm_ws"])[GL].transpose(0, 1, 3, 2))
        sh["gm_bs"] = f(np.asarray(inputs["gm_bs"])[GL])
        sh["gm_w_out"] = f(np.asarray(inputs["gm_w_out"])[GL])
    for nm in ["mla_w_in", "mla_q_norm", "mla_kv_norm", "mla_w_uq", "mla_w_ukv", "mla_w_out",
               "da_w_in", "da_lambda", "da_subln", "da_w_out"]:
        if nm in need:
            sh[nm] = f(inputs[nm])
    if "rmat" in need:
        R = np.zeros((64, 64), np.float32)
        for base in (0, 32):
            for a in range(16):
                R[base + a, base + a + 16] = -1.0
                R[base + a + 16, base + a] = 1.0
        sh["rmat"] = np.ascontiguousarray(R.T)
    return sh


def rope_tables(c, TL):
    q = c % 4
    n = TL * 128
    pos = (q * n + np.arange(n)).astype(np.int64)
    row = (pos // GRID_W).astype(np.float32)
    col = (pos % GRID_W).astype(np.float32)
    nf = 16
    inv = (np.float32(10000.0) ** (-np.arange(nf, dtype=np.float32) / np.float32(nf))).astype(np.float32)
    ang_r = (row[:, None] * inv).astype(np.float32)
    ang_c = (col[:, None] * inv).astype(np.float32)
    cr, sr, cc, sc = np.cos(ang_r), np.sin(ang_r), np.cos(ang_c), np.sin(ang_c)
    NTOK = (TL + 1) * 128
    ropeT = np.zeros((TL + 1, 128, 64), np.float32)
    ropeT[:TL] = np.concatenate([cr, sr, cc, sc], -1).reshape(TL, 128, 64)
    ropeF = np.zeros((2, 64, NTOK), np.float32)
    ropeF[0, :, :n] = np.concatenate([cr, cr, cc, cc], -1).T
    ropeF[1, :, :n] = np.concatenate([sr, sr, sc, sc], -1).T
    ropeF[0, :, n:] = 1.0
    return {"ropeT": ropeT, "ropeF": ropeF}


def prep_core(inputs, c, TL):
    b, q = c // 4, c % 4
    x = np.asarray(inputs["x"], dtype=np.float32)
    ctx = np.asarray(inputs["ctx"], dtype=np.float32)
    xin = np.zeros((TL + 1, 128, D), np.float32)
    xin[:TL] = x[b, q * TL * 128:(q + 1) * TL * 128].reshape(TL, 128, D)
    if q < 2:
        xin[TL] = ctx[b, q * 128:(q + 1) * 128]
    cc = np.stack([np.asarray(inputs["c"], np.float32)[b], np.asarray(inputs["c_ctx"], np.float32)], -1)
    cT = np.ascontiguousarray(cc.reshape(KC, 128, 2).transpose(1, 0, 2))
    return {"xin": xin, "cT": cT}


FUSED = True
TL_FULL = 16


def run_model(inputs, TL, fused, layers=(0, 1, 2, 3), ncores=8, skip=()):
    phases = ['all'] if fused else [0, 1, 2]
    state = [prep_core(inputs, c, TL) for c in range(ncores)]
    ropes = [rope_tables(c, TL) for c in range(ncores)]
    for ph in phases:
        bld = Builder(TL, ph, layers, skip)
        bld.plan()
        if not bld.steps:
            continue
        bld = Builder(TL, ph, layers, skip)
        nc = bld.build()
        sh = prep_shared(inputs, bld)
        in_maps = []
        for c in range(ncores):
            m = dict(sh)
            for nm in bld.inputs:
                if nm in state[c]:
                    m[nm] = state[c][nm]
                elif nm in ropes[c]:
                    m[nm] = ropes[c][nm]
            in_maps.append({nm: m[nm] for nm in bld.inputs})
        print("[run_model] phase", ph, "steps", bld.steps, flush=True)
        res = run_bass_kernel_spmd(nc, in_maps, core_ids=list(range(ncores)))
        outs = res.results
        for c in range(ncores):
            state[c]["xin"] = np.asarray(outs[c]["xout"])
            for nm in ("CQT", "QT"):
                if nm in outs[c]:
                    state[c][nm] = np.asarray(outs[c][nm])
        for nm_loc in list(outs[0].keys()):
            if "_loc_" not in nm_loc:
                continue
            nm_all = nm_loc.replace("_loc_", "_all_")
            for g0 in range(0, ncores, 4):
                cat = np.concatenate([np.asarray(outs[c][nm_loc]) for c in range(g0, min(g0 + 4, ncores))], axis=0)
                for c in range(g0, min(g0 + 4, ncores)):
                    state[c][nm_all] = cat
    return state


def kernel(**inputs):
    TL = TL_FULL
    state = run_model(inputs, TL, FUSED)
    x = np.asarray(inputs["x"])
    out = np.zeros(x.shape, np.float32)
    for c in range(8):
        b, q = c // 4, c % 4
        out[b, q * TL * 128:(q + 1) * TL * 128] = state[c]["xin"][:TL].reshape(TL * 128, D)
    return out
```
